# Optimizing a Trainium2 kernel written in Bass

```python
import math
import functools
import jax, jax.numpy as jnp
from jax import lax
import numpy as np

D_MODEL = 1024
BATCH = 8
SEQ = 4096
DEPTH = 1
DEC_BATCH = 32
DEC_SEQ = 8
PAST_LEN = 16384
PAGE_SIZE = 128

HEAD_DIM = 64
HEADS_PER_GROUP = 4
GROUPS = ((128, 1), (512, 4), (2048, 16))
N_GROUPS = 3
N_ATT_HEADS = N_GROUPS * HEADS_PER_GROUP
ATT_WIDTH = N_ATT_HEADS * HEAD_DIM
ATT_OUT = HEADS_PER_GROUP * HEAD_DIM
ATT_SCALE = HEAD_DIM ** -0.5
NUM_BUCKETS = 32
MAX_DISTANCE = 2048
LRU_WIDTH = 3 * D_MODEL // 4
LRU_BLOCKS = 12
LRU_BLOCK_DIM = LRU_WIDTH // LRU_BLOCKS
CONV_WIDTH = 4
LRU_C = 8.0
N_MEM = 256
MEM_HEADS = 4
MEM_HEAD_DIM = 128
MEM_WIDTH = MEM_HEADS * MEM_HEAD_DIM
MEM_SCALE = MEM_HEAD_DIM ** -0.5
D_FF = 2816
RMS_EPS = 1e-6
IN_COLS = 3 * ATT_WIDTH + 2 * LRU_WIDTH + MEM_WIDTH + 3 * D_MODEL

kernel_name = "hybrid_dilated_rglru_memory_decoder_step"


def rmsnorm(x, g):
    x32 = x.astype(jnp.float32)
    y = x32 * lax.rsqrt(jnp.mean(x32 * x32, axis=-1, keepdims=True) + RMS_EPS)
    return (y * g.astype(jnp.float32)).astype(x.dtype)


def swiglu(x, w_gu, w_down):
    gate, up = jnp.split(x @ w_gu, 2, axis=-1)
    return (jax.nn.silu(gate) * up) @ w_down


def rel_bucket(dist):
    n = jnp.maximum(dist, 0)
    max_exact = NUM_BUCKETS // 2
    nf = jnp.maximum(n, 1).astype(jnp.float32)
    large = max_exact + (jnp.log(nf / max_exact) / math.log(MAX_DISTANCE / max_exact)
                         * (NUM_BUCKETS - max_exact)).astype(jnp.int32)
    large = jnp.minimum(large, NUM_BUCKETS - 1)
    return jnp.where(n < max_exact, n, large)


def split_points():
    sizes = [ATT_WIDTH] * 3 + [LRU_WIDTH] * 2 + [MEM_WIDTH] + [D_MODEL] * 2
    pts, acc = [], 0
    for s in sizes:
        acc += s
        pts.append(acc)
    return pts


def dilated_group_prompt(q, k, v, bias_tab, window, dil):
    B, S, H, E = q.shape
    n_rel = window // dil
    L = S // dil
    nb = -(-L // n_rel)
    Lp = nb * n_rel

    def to_blocks(t):
        t = t.reshape(B, L, dil, H, E)
        t = jnp.pad(t, ((0, 0), (0, Lp - L), (0, 0), (0, 0), (0, 0)))
        return t.reshape(B, nb, n_rel, dil, H, E)

    def with_prev(t):
        prev = jnp.pad(t, ((0, 0), (1, 0), (0, 0), (0, 0), (0, 0), (0, 0)))[:, :nb]
        return jnp.concatenate([prev, t], axis=2)

    qb = to_blocks(q)
    kc = with_prev(to_blocks(k))
    vc = with_prev(to_blocks(v))
    qi = jnp.arange(n_rel)[:, None]
    ki = jnp.arange(2 * n_rel)[None, :]
    delta = qi - ki + n_rel
    band = (delta >= 0) & (delta <= n_rel)
    valid = band[None] & ((jnp.arange(nb)[:, None, None] * n_rel + ki[None] - n_rel) >= 0)
    bias = jnp.transpose(bias_tab[rel_bucket(dil * delta)], (2, 0, 1)).astype(jnp.float32)
    s = jnp.einsum('bnqrhe,bnkrhe->bnrhqk', qb, kc).astype(jnp.float32) * ATT_SCALE + bias
    s = jnp.where(valid[None, :, None, None], s, -jnp.inf)
    m = jnp.max(s, axis=-1, keepdims=True)
    p = jnp.exp(s - m)
    l = jnp.sum(p, axis=-1)
    o = jnp.einsum('bnrhqk,bnkrhe->bnqrhe', p.astype(v.dtype), vc).astype(jnp.float32)
    o = o / jnp.transpose(l, (0, 1, 4, 2, 3))[..., None]
    lse = jnp.transpose(m[..., 0] + jnp.log(l), (0, 1, 4, 2, 3))
    o = o.reshape(B, Lp, dil, H, E)[:, :L].reshape(B, S, H, E)
    lse = lse.reshape(B, Lp, dil, H)[:, :L].reshape(B, S, H)
    return o, lse


def dilated_group_sample(q, k, v, k_buf, v_buf, bias_tab, window, dil):
    Lc = k_buf.shape[1]
    T = q.shape[1]
    n_rel = window // dil
    kc = jnp.concatenate([k_buf, k.astype(k_buf.dtype)], axis=1)
    vc = jnp.concatenate([v_buf, v.astype(v_buf.dtype)], axis=1)
    steps = jnp.arange(n_rel + 1)
    idx = (Lc + jnp.arange(T))[:, None] - dil * steps[None, :]
    valid = idx >= 0
    idx = jnp.maximum(idx, 0)
    kg = kc[:, idx]
    vg = vc[:, idx]
    bias = bias_tab[rel_bucket(dil * steps)].T.astype(jnp.float32)
    s = jnp.einsum('bthe,btmhe->bhtm', q, kg).astype(jnp.float32) * ATT_SCALE + bias[:, None, :]
    s = jnp.where(valid[None, None], s, -jnp.inf)
    m = jnp.max(s, axis=-1, keepdims=True)
    p = jnp.exp(s - m)
    l = jnp.sum(p, axis=-1)
    o = jnp.einsum('bhtm,btmhe->bthe', p.astype(vg.dtype), vg).astype(jnp.float32)
    o = o / jnp.transpose(l, (0, 2, 1))[..., None]
    lse = jnp.transpose(m[..., 0] + jnp.log(l), (0, 2, 1))
    return o, lse, kc[:, T:], vc[:, T:]


def combine_groups(outs, lses):
    wts = jax.nn.softmax(jnp.stack(lses, axis=0), axis=0)
    o = jnp.einsum('gblh,gblhe->blhe', wts, jnp.stack(outs, axis=0))
    return o.reshape(o.shape[0], o.shape[1], ATT_OUT)


def attend_prompt(q, k, v, rel_bias):
    S = q.shape[1]
    outs, lses, states = [], [], []
    for g, (window, dil) in enumerate(GROUPS):
        sl = slice(g * HEADS_PER_GROUP, (g + 1) * HEADS_PER_GROUP)
        o, lse = dilated_group_prompt(q[:, :, sl], k[:, :, sl], v[:, :, sl], rel_bias[:, sl], window, dil)
        outs.append(o)
        lses.append(lse)
        keep = min(window, S)
        states += [k[:, S - keep:, sl], v[:, S - keep:, sl]]
    return combine_groups(outs, lses), states


def attend_sample(q, k, v, rel_bias, bufs):
    outs, lses, states = [], [], []
    for g, (window, dil) in enumerate(GROUPS):
        sl = slice(g * HEADS_PER_GROUP, (g + 1) * HEADS_PER_GROUP)
        o, lse, kb, vb = dilated_group_sample(q[:, :, sl], k[:, :, sl], v[:, :, sl],
                                              bufs[2 * g], bufs[2 * g + 1], rel_bias[:, sl], window, dil)
        outs.append(o)
        lses.append(lse)
        states += [kb, vb]
    return combine_groups(outs, lses), states


def causal_conv(x, buf, w, b):
    xp = jnp.concatenate([buf.astype(x.dtype), x], axis=1)
    y = lax.conv_general_dilated(xp, w[:, None, :].astype(x.dtype), window_strides=(1,), padding='VALID',
                                 dimension_numbers=('NWC', 'WIO', 'NWC'),
                                 feature_group_count=x.shape[-1])
    return y + b, xp[:, xp.shape[1] - (CONV_WIDTH - 1):]


def rglru(x, h0, w_a, b_a, w_i, b_i, lam):
    B, L, R = x.shape
    x32 = x.astype(jnp.float32)
    xb = x32.reshape(B, L, LRU_BLOCKS, LRU_BLOCK_DIM)
    r = jax.nn.sigmoid(jnp.einsum('blne,nef->blnf', xb, w_a.astype(jnp.float32)).reshape(B, L, R) + b_a)
    i = jax.nn.sigmoid(jnp.einsum('blne,nef->blnf', xb, w_i.astype(jnp.float32)).reshape(B, L, R) + b_i)
    log_a = -LRU_C * r * jax.nn.softplus(-lam.astype(jnp.float32))
    a = jnp.exp(log_a)
    u = jnp.sqrt(-jnp.expm1(2.0 * log_a)) * (i * x32)

    def step(h, au):
        a_t, u_t = au
        h = a_t * h + u_t
        return h, h

    h_last, hs = lax.scan(step, h0.astype(jnp.float32), (jnp.swapaxes(a, 0, 1), jnp.swapaxes(u, 0, 1)))
    return jnp.swapaxes(hs, 0, 1).astype(x.dtype), h_last.astype(h0.dtype)


def memory_kv(mem, g, w_kv):
    B, M, _ = mem.shape
    mk, mv = jnp.split(rmsnorm(mem, g) @ w_kv, 2, axis=-1)
    return mk.reshape(B, M, MEM_HEADS, MEM_HEAD_DIM), mv.reshape(B, M, MEM_HEADS, MEM_HEAD_DIM)


def memory_attention(q, mk, mv):
    B, L = q.shape[0], q.shape[1]
    s = jnp.einsum('blhe,bmhe->bhlm', q, mk.astype(q.dtype)).astype(jnp.float32) * MEM_SCALE
    p = jax.nn.softmax(s, axis=-1).astype(q.dtype)
    o = jnp.einsum('bhlm,bmhe->blhe', p, mv.astype(q.dtype))
    return o.reshape(B, L, MEM_WIDTH)


def decoder_layer(x, attend, conv_buf, lru_h0, mk, mv, rel_bias, lp):
    B, L, _ = x.shape
    x = x + 0.5 * swiglu(rmsnorm(x, lp['ffn1_norm']), lp['ffn1_w_gu'], lp['ffn1_w_down'])
    h = rmsnorm(x, lp['mix_norm'])
    z = h @ lp['w_in']
    q, k, v, xr, yr, qm, ga, gb, gc = jnp.split(z, split_points(), axis=-1)
    bg_a, bg_b, bg_c = jnp.split(lp['b_gate'], 3)
    y_att, att_state = attend(q.reshape(B, L, N_ATT_HEADS, HEAD_DIM),
                              k.reshape(B, L, N_ATT_HEADS, HEAD_DIM),
                              v.reshape(B, L, N_ATT_HEADS, HEAD_DIM), rel_bias)
    y_att = y_att.astype(x.dtype)
    xc, conv_new = causal_conv(xr, conv_buf, lp['conv_w'], lp['conv_b'])
    hr, lru_new = rglru(xc, lru_h0, lp['lru_w_a'], lp['lru_b_a'], lp['lru_w_i'], lp['lru_b_i'], lp['lru_lambda'])
    y_rec = jax.nn.gelu(yr) * hr
    y_mem = memory_attention(qm.reshape(B, L, MEM_HEADS, MEM_HEAD_DIM), mk, mv)
    merged = (jax.nn.sigmoid(ga + bg_a) * (y_att @ lp['w_att_o'])
              + jax.nn.sigmoid(gb + bg_b) * (y_rec @ lp['w_rec_o'])
              + jax.nn.sigmoid(gc + bg_c) * (y_mem @ lp['w_mem_o']))
    x = x + merged @ lp['w_out']
    x = x + 0.5 * swiglu(rmsnorm(x, lp['ffn2_norm']), lp['ffn2_w_gu'], lp['ffn2_w_down'])
    return x, att_state, conv_new, lru_new


def setup_inputs(seed: int = 0) -> dict:
    key = jax.random.key(seed)
    keys = list(jax.random.split(key, 48))
    f32 = jnp.float32

    def nrm(shape, scale):
        return scale * jax.random.normal(keys.pop(), shape, f32)

    def gain(shape):
        return 1.0 + 0.01 * jax.random.normal(keys.pop(), shape, f32)

    lc = [min(w, PAST_LEN) for w, _ in GROUPS]
    u = jax.random.uniform(keys.pop(), (DEPTH, LRU_WIDTH), f32, 0.9, 0.999)
    lru_lambda = jnp.log(u) - jnp.log1p(-u)
    return {
        'x_prompt': nrm((BATCH, SEQ, D_MODEL), 1.0),
        'x_sample': nrm((DEC_BATCH, DEC_SEQ, D_MODEL), 1.0),
        'mem_prompt': nrm((BATCH, N_MEM, D_MODEL), 1.0),
        'cache_win_k0': nrm((DEPTH, DEC_BATCH, lc[0], HEADS_PER_GROUP, HEAD_DIM), 1.0),
        'cache_win_v0': nrm((DEPTH, DEC_BATCH, lc[0], HEADS_PER_GROUP, HEAD_DIM), 1.0),
        'cache_win_k1': nrm((DEPTH, DEC_BATCH, lc[1], HEADS_PER_GROUP, HEAD_DIM), 1.0),
        'cache_win_v1': nrm((DEPTH, DEC_BATCH, lc[1], HEADS_PER_GROUP, HEAD_DIM), 1.0),
        'cache_win_k2': nrm((DEPTH, DEC_BATCH, lc[2], HEADS_PER_GROUP, HEAD_DIM), 1.0),
        'cache_win_v2': nrm((DEPTH, DEC_BATCH, lc[2], HEADS_PER_GROUP, HEAD_DIM), 1.0),
        'state_conv': nrm((DEPTH, DEC_BATCH, CONV_WIDTH - 1, LRU_WIDTH), 1.0),
        'state_lru': nrm((DEPTH, DEC_BATCH, LRU_WIDTH), 0.5),
        'cache_mem_k': nrm((DEPTH, DEC_BATCH, N_MEM, MEM_HEADS, MEM_HEAD_DIM), 1.0),
        'cache_mem_v': nrm((DEPTH, DEC_BATCH, N_MEM, MEM_HEADS, MEM_HEAD_DIM), 1.0),
        'rel_bias': nrm((NUM_BUCKETS, N_ATT_HEADS), 0.5),
        'ffn1_norm': gain((DEPTH, D_MODEL)),
        'ffn1_w_gu': nrm((DEPTH, D_MODEL, 2 * D_FF), D_MODEL ** -0.5),
        'ffn1_w_down': nrm((DEPTH, D_FF, D_MODEL), D_FF ** -0.5),
        'mix_norm': gain((DEPTH, D_MODEL)),
        'w_in': nrm((DEPTH, D_MODEL, IN_COLS), D_MODEL ** -0.5),
        'b_gate': nrm((DEPTH, 3 * D_MODEL), 0.02),
        'conv_w': nrm((DEPTH, CONV_WIDTH, LRU_WIDTH), CONV_WIDTH ** -0.5),
        'conv_b': nrm((DEPTH, LRU_WIDTH), 0.02),
        'lru_w_a': nrm((DEPTH, LRU_BLOCKS, LRU_BLOCK_DIM, LRU_BLOCK_DIM), LRU_BLOCK_DIM ** -0.5),
        'lru_b_a': nrm((DEPTH, LRU_WIDTH), 0.02),
        'lru_w_i': nrm((DEPTH, LRU_BLOCKS, LRU_BLOCK_DIM, LRU_BLOCK_DIM), LRU_BLOCK_DIM ** -0.5),
        'lru_b_i': nrm((DEPTH, LRU_WIDTH), 0.02),
        'lru_lambda': lru_lambda,
        'mem_norm': gain((DEPTH, D_MODEL)),
        'w_mem_kv': nrm((DEPTH, D_MODEL, 2 * MEM_WIDTH), D_MODEL ** -0.5),
        'w_att_o': nrm((DEPTH, ATT_OUT, D_MODEL), ATT_OUT ** -0.5),
        'w_rec_o': nrm((DEPTH, LRU_WIDTH, D_MODEL), LRU_WIDTH ** -0.5),
        'w_mem_o': nrm((DEPTH, MEM_WIDTH, D_MODEL), MEM_WIDTH ** -0.5),
        'w_out': nrm((DEPTH, D_MODEL, D_MODEL), D_MODEL ** -0.5),
        'ffn2_norm': gain((DEPTH, D_MODEL)),
        'ffn2_w_gu': nrm((DEPTH, D_MODEL, 2 * D_FF), D_MODEL ** -0.5),
        'ffn2_w_down': nrm((DEPTH, D_FF, D_MODEL), D_FF ** -0.5),
        'final_norm': gain((D_MODEL,)),
    }


def reference(x_prompt, x_sample, mem_prompt,
              cache_win_k0, cache_win_v0, cache_win_k1, cache_win_v1, cache_win_k2, cache_win_v2,
              state_conv, state_lru, cache_mem_k, cache_mem_v,
              rel_bias,
              ffn1_norm, ffn1_w_gu, ffn1_w_down,
              mix_norm, w_in, b_gate,
              conv_w, conv_b, lru_w_a, lru_b_a, lru_w_i, lru_b_i, lru_lambda,
              mem_norm, w_mem_kv,
              w_att_o, w_rec_o, w_mem_o, w_out,
              ffn2_norm, ffn2_w_gu, ffn2_w_down,
              final_norm):
    xp, xs = x_prompt, x_sample
    Bp = xp.shape[0]
    p_att, p_conv, p_lru, p_mk, p_mv = [], [], [], [], []
    s_att, s_conv, s_lru = [], [], []
    for l in range(DEPTH):
        lp = dict(ffn1_norm=ffn1_norm[l], ffn1_w_gu=ffn1_w_gu[l], ffn1_w_down=ffn1_w_down[l],
                  mix_norm=mix_norm[l], w_in=w_in[l], b_gate=b_gate[l],
                  conv_w=conv_w[l], conv_b=conv_b[l], lru_w_a=lru_w_a[l], lru_b_a=lru_b_a[l],
                  lru_w_i=lru_w_i[l], lru_b_i=lru_b_i[l], lru_lambda=lru_lambda[l],
                  w_att_o=w_att_o[l], w_rec_o=w_rec_o[l], w_mem_o=w_mem_o[l], w_out=w_out[l],
                  ffn2_norm=ffn2_norm[l], ffn2_w_gu=ffn2_w_gu[l], ffn2_w_down=ffn2_w_down[l])
        mk, mv = memory_kv(mem_prompt, mem_norm[l], w_mem_kv[l])
        conv0 = jnp.zeros((Bp, CONV_WIDTH - 1, LRU_WIDTH), xp.dtype)
        h0 = jnp.zeros((Bp, LRU_WIDTH), xp.dtype)
        xp, att_st, conv_st, lru_st = decoder_layer(xp, attend_prompt, conv0, h0, mk, mv, rel_bias, lp)
        p_att.append(att_st)
        p_conv.append(conv_st)
        p_lru.append(lru_st)
        p_mk.append(mk)
        p_mv.append(mv)
        bufs = (cache_win_k0[l], cache_win_v0[l], cache_win_k1[l], cache_win_v1[l],
                cache_win_k2[l], cache_win_v2[l])
        xs, att_st, conv_st, lru_st = decoder_layer(xs, functools.partial(attend_sample, bufs=bufs),
                                                    state_conv[l], state_lru[l],
                                                    cache_mem_k[l], cache_mem_v[l], rel_bias, lp)
        s_att.append(att_st)
        s_conv.append(conv_st)
        s_lru.append(lru_st)
    y_prompt = rmsnorm(xp, final_norm)
    y_sample = rmsnorm(xs, final_norm)
    p_win_k0, p_win_v0, p_win_k1, p_win_v1, p_win_k2, p_win_v2 = [
        jnp.stack([st[i] for st in p_att]) for i in range(2 * N_GROUPS)]
    s_win_k0, s_win_v0, s_win_k1, s_win_v1, s_win_k2, s_win_v2 = [
        jnp.stack([st[i] for st in s_att]) for i in range(2 * N_GROUPS)]
    p_conv_out = jnp.stack(p_conv)
    p_lru_out = jnp.stack(p_lru)
    p_mem_k = jnp.stack(p_mk)
    p_mem_v = jnp.stack(p_mv)
    s_conv_out = jnp.stack(s_conv)
    s_lru_out = jnp.stack(s_lru)
    return (y_prompt, y_sample,
            p_win_k0, p_win_v0, p_win_k1, p_win_v1, p_win_k2, p_win_v2,
            p_conv_out, p_lru_out, p_mem_k, p_mem_v,
            s_win_k0, s_win_v0, s_win_k1, s_win_v1, s_win_k2, s_win_v2,
            s_conv_out, s_lru_out)
```

```python
import contextlib
import sys
import numpy as np
import concourse.bass as bass
import concourse.mybir as mybir
from concourse.bass_utils import run_bass_kernel_spmd

F32 = mybir.dt.float32
BF16 = mybir.dt.bfloat16
AF = mybir.ActivationFunctionType
ALU = mybir.AluOpType

PROCS = ("pe", "act", "dve", "pool", "sp")
SAME_ENGINE_SYNC = True


class _Op:
    __slots__ = ("proc", "fn", "seq", "signal", "sigidx", "lane", "lanecnt", "waits", "snap", "is_dma", "where")


class Prog:
    def __init__(self, nc):
        self.nc = nc
        self.ops = {p: [] for p in PROCS}
        self.last_w = {}
        self.readers = {}
        self.clock = {p: {} for p in PROCS}
        self.lane_last = {}
        self.lane_proc = {}
        self.rr = {}
        self.pending = {}

    def barrier(self):
        for p in PROCS:
            lst = [self.ops[q][-1] for q in PROCS if q != p and self.ops[q] and not self.ops[q][-1].is_dma]
            for q in PROCS:
                if q != p:
                    for o in reversed(self.ops[q]):
                        if not o.is_dma:
                            lst.append(o)
                            break
            lst += list(self.lane_last.values())
            self.pending[p] = lst

    def _dep(self, op, y):
        p = op.proc
        if y.is_dma:
            src, val = y.lane, y.lanecnt
        else:
            src, val = y.proc, y.seq
            if y.proc == p and (p == "pe" or not SAME_ENGINE_SYNC):
                return
        ck = self.clock[p]
        if ck.get(src, -1) >= val:
            return
        if not y.is_dma:
            y.signal = True
        op.waits.append((src, val, y))
        ck[src] = val
        for s, v in y.snap.items():
            if ck.get(s, -1) < v:
                ck[s] = v

    def add(self, proc, fn, reads=(), writes=(), lane=None):
        op = _Op()
        op.proc = proc
        op.fn = fn
        op.seq = len(self.ops[proc])
        op.signal = False
        op.sigidx = None
        op.lane = lane
        op.lanecnt = None
        op.waits = []
        op.is_dma = lane is not None
        f_ = sys._getframe(1)
        w_ = []
        while f_ is not None and len(w_) < 4:
            w_.append(f_.f_lineno)
            f_ = f_.f_back
        op.where = w_
        writes = list(writes) + [k for k in reads if isinstance(k, tuple) and k[0] == "ps" and k not in writes]
        deps = []
        for k in reads:
            w = self.last_w.get(k)
            if w is not None:
                deps.append(w)
        for k in writes:
            w = self.last_w.get(k)
            if w is not None:
                deps.append(w)
            deps.extend(self.readers.get(k, ()))
        deps.extend(self.pending.pop(proc, ()))
        if lane is not None:
            prev = self.lane_last.get(lane)
            if prev is not None:
                deps.append(prev)
                op.lanecnt = prev.lanecnt + 1
            else:
                op.lanecnt = 1
            assert self.lane_proc.setdefault(lane, proc) == proc
        seen = set()
        for y in sorted(deps, key=lambda o: -o.seq):
            if id(y) in seen:
                continue
            seen.add(id(y))
            self._dep(op, y)
        op.snap = dict(self.clock[proc])
        self.ops[proc].append(op)
        if lane is not None:
            self.lane_last[lane] = op
        for k in reads:
            self.readers.setdefault(k, []).append(op)
        for k in writes:
            self.last_w[k] = op
            self.readers[k] = []
        return op

    def dma(self, proc, out, in_, reads=(), writes=(), lanes=None, **kw):
        if lanes is None:
            lanes = [f"{proc}{i}" for i in range(8)]
        key = tuple(lanes)
        i = self.rr.get(key, 0)
        self.rr[key] = i + 1
        lane = lanes[i % len(lanes)]
        return self.add(proc, lambda e: e.dma_start(out=out, in_=in_, **kw), reads, writes, lane=lane)

    def emit(self, st):
        nc = self.nc
        fin = _Op()
        fin.proc = "sp"; fin.fn = None; fin.seq = len(self.ops["sp"]); fin.signal = False
        fin.sigidx = None; fin.lane = None; fin.lanecnt = None; fin.waits = []; fin.is_dma = False; fin.where = []
        for lane, y in self.lane_last.items():
            self._dep(fin, y)
        for p in PROCS:
            if p != "sp" and self.ops[p]:
                self._dep(fin, self.ops[p][-1])
        fin.snap = {}
        self.ops["sp"].append(fin)
        for p in PROCS:
            n = 0
            for op in self.ops[p]:
                if op.signal and not op.is_dma:
                    n += 1
                    op.sigidx = n
        sems = {}
        for p in PROCS:
            sems[p] = st.enter_context(nc.semaphore(f"s_{p}"))
        for lane in self.lane_last:
            sems[lane] = st.enter_context(nc.semaphore(f"l_{lane}"))
        block = st.enter_context(nc.Block())

        def run(proc):
            def body(eng):
                for op in self.ops[proc]:
                    for (src, val, y) in op.waits:
                        if y.is_dma:
                            eng.wait_ge(sems[src], 16 * val)
                        else:
                            eng.wait_ge(sems[src], y.sigidx)
                    if op.fn is None:
                        continue
                    try:
                        ins = op.fn(eng)
                    except Exception:
                        print("EMIT FAILED for op at lines", op.where, "proc", op.proc)
                        raise
                    if op.is_dma:
                        ins.then_inc(sems[op.lane], 16)
                    elif op.signal:
                        ins.then_inc(sems[proc], 1)
            return body

        block.tensor(run("pe"))
        block.scalar(run("act"))
        block.vector(run("dve"))
        block.gpsimd(run("pool"))
        block.sync(run("sp"))

    def stats(self):
        return {p: (len(self.ops[p]), sum(1 for o in self.ops[p] if o.signal), sum(len(o.waits) for o in self.ops[p])) for p in PROCS}


D = 1024
DFF = 2816
SEQ = 4096
TN = 512
NT = SEQ // TN
NSB = 4
ST = 8
NS = NSB * ST
NEG = -30000.0
NW = 4
Q0, K0, V0c, XR0, YR0, QM0, GA0, GB0, GC0 = 0, 768, 1536, 2304, 3072, 3840, 4352, 5376, 6400
R_G1, R_GM, R_G2, R_GMEM, R_BG, R_CW, R_CB, R_BA, R_BI, R_LAM, NROWS = 0, 8, 16, 24, 32, 56, 80, 86, 92, 98, 104
GROUPS = ((128, 1), (512, 4), (2048, 16))


def _onehot_const():
    oh = np.zeros((4, 33, 384), np.float32)
    oh[3, 32, :] = NEG
    oh[3, 0, 127] = 1.0
    oh[3, 32, 127] = 0.0
    for g, (_, d) in enumerate(GROUPS):
        n = np.arange(129, dtype=np.int64) * d
        nf = np.maximum(n, 1).astype(np.float32)
        large = 16 + (np.log(nf / np.float32(16)) / np.float32(np.log(128.0)) * np.float32(16)).astype(np.int32)
        large = np.minimum(large, 31)
        bk = np.where(n < 16, n, large)
        oh[g, 32, :] = NEG
        for delta in range(129):
            oh[g, int(bk[delta]), 127 + delta] = 1.0
            oh[g, 32, 127 + delta] = 0.0
    return oh


class _Stop(Exception):
    pass


def build_program(upto=None, ntiles=NT):
    nc = bass.Bass("TRN2", target_bir_lowering=False)
    P = Prog(nc)
    st = contextlib.ExitStack()

    def din(name, shape):
        return nc.dram_tensor(name, list(shape), F32, kind="ExternalInput").ap()

    def dout(name, shape):
        return nc.dram_tensor(name, list(shape), F32, kind="ExternalOutput").ap()

    def sb(name, shape, dt):
        return st.enter_context(nc.sbuf_tensor("sb_" + name, list(shape), dt))

    x_d = din("x", [SEQ, D]); xs_d = din("xs", [NS, D]); mem_d = din("mem", [256, D])
    ck_d = [din(f"ck{g}", [NSB, GROUPS[g][0], 256]) for g in range(3)]
    cv_d = [din(f"cv{g}", [NSB, GROUPS[g][0], 256]) for g in range(3)]
    sconv_d = din("sconv", [NSB * 3 * 6, 128]); slru_d = din("slru", [NSB * 6, 128])
    cmk_d = din("cmk", [NSB, 256, 512]); cmv_d = din("cmv", [NSB, 256, 512])
    relb_d = din("rel_bias", [32, 12]); vrows_d = din("vecrows", [NROWS, 128]); gfin_d = din("final_norm", [1, D])
    w1gu_d = din("ffn1_w_gu", [D, 2 * DFF]); w1d_d = din("ffn1_w_down", [DFF, D])
    win_d = din("w_in", [D, 7424])
    lwa_d = din("lru_w_a", [12, 64, 64]); lwi_d = din("lru_w_i", [12, 64, 64])
    wmkv_d = din("w_mem_kv", [D, D]); wao_d = din("w_att_o", [256, D]); wro_d = din("w_rec_o", [768, D])
    wmo_d = din("w_mem_o", [512, D]); wout_d = din("w_out", [D, D])
    w2gu_d = din("ffn2_w_gu", [D, 2 * DFF]); w2d_d = din("ffn2_w_down", [DFF, D])
    identf_d = din("identf", [128, 128]); jflip_d = din("jflip", [128, 128]); oh_d = din("oh", [4, 33, 384])

    y_d = dout("y", [SEQ, D]); ys_d = dout("ys", [NS, D])
    pk_d = [dout(f"pk{g}", [min(GROUPS[g][0], SEQ), 256]) for g in range(3)]
    pv_d = [dout(f"pv{g}", [min(GROUPS[g][0], SEQ), 256]) for g in range(3)]
    pconv_d = dout("pconv", [3, 768]); plru_d = dout("plru", [6, 128])
    pmk_d = dout("pmk", [256, 512]); pmv_d = dout("pmv", [256, 512])
    sk_d = [dout(f"sk{g}", [NSB, GROUPS[g][0], 256]) for g in range(3)]
    sv_d = [dout(f"sv{g}", [NSB, GROUPS[g][0], 256]) for g in range(3)]
    sconvo_d = dout("sconvo", [NSB, 3, 768]); slruo_d = dout("slruo", [NSB * 6, 128])
    fx_d = nc.dram_tensor("fx_scratch", [20, 384], F32, kind="Internal").ap()
    qs_d = nc.dram_tensor("qs_scratch", [NS, 768], F32, kind="Internal").ap()

    wsl = [sb(f"wsl{i}", [128, 8, 512], BF16) for i in range(NW)]
    xres = [sb("xres0", [128, 4, D], F32)]
    AT = sb("actT", [128, 22 * 512], BF16)
    junk = AT[:, 20 * 512:22 * 512]
    JK = [("actT", 20), ("actT", 21)]
    ystage = [sb(f"ystage{i}", [128, D], F32) for i in range(2)]
    hT = sb("hT", [128, 8, 512], BF16)
    QM = sb("QM", [128, 8, 512], BF16)
    k01 = sb("k01", [128, 4, 2, 512], BF16)
    k2 = sb("k2", [128, 2, SEQ], BF16)
    V0 = sb("V0", [128, 2, 4, 256], BF16)
    V1 = sb("V1", [128, 2, 4, 256], BF16)
    V2 = sb("V2", [128, 2, 16, 256], BF16)
    V2s = sb("V2s", [128, 4, 256], BF16)
    biasT = sb("biasT", [128, 3, 2, 4, 128], F32)
    scr = [sb(f"scr{i}", [128, 512], F32) for i in range(8)]
    pbb = [sb(f"pb{i}", [128, 512], BF16) for i in range(4)]
    XR = sb("XR", [128, 6, 3 + 512], F32)
    qmT = sb("qmT", [128, 4, 512], BF16)
    mkT = sb("mkT", [128, 4, 256], BF16)
    mvb = sb("mvb", [128, 2, 4, 128], BF16)
    gfin = sb("gfin", [128, D], F32)
    colv = sb("colv", [128, NROWS], F32)
    dcol = sb("dcol", [128, 64], F32)
    vrows = sb("vrows", [NROWS, 128], F32)
    identf = sb("identf", [128, 128], F32)
    identb = sb("identb", [128, 128], BF16)
    jflip = sb("jflipb", [128, 128], F32)
    onesb = sb("onesb", [128, 128], BF16)
    wa_bd = sb("wa_bd", [128, 6, 128], BF16)
    wi_bd = sb("wi_bd", [128, 6, 128], BF16)
    stat = sb("stat", [128, 32], F32)
    chalf = sb("chalf", [128, 8], F32)
    hst = sb("hst", [128, 6, NSB], F32)
    rb33 = sb("rb33", [33, 12], F32)
    sbias = sb("sbias", [128, 40], F32)
    sbiasN = sb("sbiasN", [8, 96], F32)
    onesf = sb("onesf", [128, 64], F32)
    ps = [st.enter_context(nc.psum_tensor(f"ps{i}", [128, 512], F32)) for i in range(8)]

    def actc(c):
        return AT[:, c * 512:(c + 1) * 512]

    def bc_mid(ap2, n):
        (p0, pn), (s1, m) = ap2.ap
        return bass.AP(ap2.tensor, ap2.offset, [[p0, pn], [0, n], [s1, m]])

    gen_i = [0]

    def genbank():
        b = gen_i[0] % 4
        gen_i[0] += 1
        return b

    scr_i = [0]

    def nscr(lo=0, hi=8):
        i = lo + scr_i[0] % (hi - lo)
        scr_i[0] += 1
        return i

    def mm(out, lhsT, rhs, start, stop, reads, writes):
        P.add("pe", lambda e: e.matmul(out, lhsT, rhs, start=start, stop=stop, skip_group_check=True), reads, writes)

    def act(out, in_, func, reads, writes, **kw):
        P.add("act", lambda e: e.activation(out=out, in_=in_, func=func, **kw), reads, writes)

    def dve_ts(out, in0, s1, s2, op0, op1, reads, writes):
        if op1 is None:
            P.add("dve", lambda e: e.tensor_scalar(out=out, in0=in0, scalar1=s1, scalar2=None, op0=op0), reads, writes)
        else:
            P.add("dve", lambda e: e.tensor_scalar(out=out, in0=in0, scalar1=s1, scalar2=s2, op0=op0, op1=op1), reads, writes)

    def dve_stt(out, in0, scalar, in1, op0, op1, reads, writes):
        P.add("dve", lambda e: e.scalar_tensor_tensor(out=out, in0=in0, scalar=scalar, in1=in1, op0=op0, op1=op1), reads, writes)

    def dve_tt(out, in0, in1, op, reads, writes):
        P.add("dve", lambda e: e.tensor_tensor(out=out, in0=in0, in1=in1, op=op), reads, writes)

    def dve_copy(out, in_, reads, writes):
        P.add("dve", lambda e: e.tensor_copy(out=out, in_=in_), reads, writes)

    ev_i = [0]

    def evac(out, in_, reads, writes, scale=None):
        ev_i[0] += 1
        if ev_i[0] % 2 == 0:
            if scale is None:
                act(out, in_, AF.Copy, reads, writes)
            else:
                act(out, in_, AF.Copy, reads, writes, scale=scale)
        else:
            if scale is None:
                dve_copy(out, in_, reads, writes)
            else:
                dve_ts(out, in_, scale, None, ALU.mult, None, reads, writes)

    widx = [0]

    def wload(parts):
        s = widx[0] % NW
        widx[0] += 1
        for pi, (src, off, nk, ncol, kp) in enumerate(parts):
            dst = wsl[s][0:kp, 0:nk, off:off + ncol]
            P.dma("pool", dst, src.rearrange("(k p) c -> p k c", p=kp), writes=[("w", s, pi)], lanes=[f"w{s}_{pi}"])
        return s

    def wk(s):
        return [("w", s, 0), ("w", s, 1)]

    def xk(tb):
        return [("x", tb, 0), ("x", tb, 1)]

    C_HBG, C_HBA, C_HBI, C_M4SP = 0, 24, 30, 36

    P.marks = []

    def CK(name):
        P.marks.append((name, len(P.ops["pe"])))
        if upto == name:
            raise _Stop()

    def setup():
      if True:
        P.dma("sp", identf[:], identf_d, writes=["identf"])
        P.dma("sp", jflip[:], jflip_d, writes=["jflip"])
        P.dma("sp", vrows[:], vrows_d, writes=["vrows"])
        P.dma("sp", gfin[:], bass.AP(gfin_d.tensor, 0, [[0, 128], [1, D]]), writes=["gfin"])
        P.add("dve", lambda e: e.memset(rb33[:], 1.0), writes=["rb33"])
        P.dma("sp", rb33[0:32, :], relb_d, reads=[], writes=["rb33"])
        P.add("dve", lambda e: e.memset(onesb[:], 1.0), writes=["onesb"])
        P.add("dve", lambda e: e.memset(chalf[:, :], -0.5), writes=["chalf"])
        P.add("dve", lambda e: e.memset(hst[:], 0.0), writes=["hst"])
        P.add("dve", lambda e: e.memset(XR[:], 0.0), writes=["XR"])
        P.add("pool", lambda e: e.memset(wa_bd[:], 0.0), writes=["wa_bd"])
        P.add("pool", lambda e: e.memset(wi_bd[:], 0.0), writes=["wi_bd"])
        dve_copy(identb[:], identf[:], ["identf"], ["identb"])
        CK("s1")
        for bb in range(2):
            for (dst, src, nm) in ((wa_bd, lwa_d, "wa_bd"), (wi_bd, lwi_d, "wi_bd")):
                P.dma("pool", dst[64 * bb:64 * bb + 64, :, 64 * bb:64 * bb + 64],
                      src.rearrange("(c b) e f -> b e c f", b=2)[bb], writes=[nm], lanes=[f"lw{bb}{nm}"])
        CK("s2")
        mm(ps[0][:, 0:NROWS], vrows[:, :], identf[0:NROWS, 0:NROWS], True, True, ["vrows", "identf"], [("ps", 0)])
        dve_copy(colv[:], ps[0][:, 0:NROWS], [("ps", 0)], ["colv"])
        dve_ts(dcol[:, 0:24], colv[:, R_BG:R_BG + 24], 0.5, None, ALU.mult, None, ["colv"], ["dcol"])
        dve_ts(dcol[:, 24:36], colv[:, R_BA:R_BA + 12], 0.5, None, ALU.mult, None, ["colv"], ["dcol"])
        act(dcol[:, 42:48], colv[:, R_LAM:R_LAM + 6], AF.Exp, ["colv"], ["dcol2"], scale=-1.0)
        act(dcol[:, 48:54], dcol[:, 42:48], AF.Ln, ["dcol2"], ["dcol3"], bias=1.0)
        dve_ts(dcol[:, 36:42], dcol[:, 48:54], -4.0, None, ALU.mult, None, ["dcol3"], ["dcol"])

        CK("s3")

    def setup_bias():
      if True:
        for g in range(3):
            so = nscr()
            P.dma("sp", scr[so][0:33, 0:384], oh_d[g], writes=[("scr", so)])
            mm(ps[1][0:4, 0:384], rb33[:, 4 * g:4 * g + 4], scr[so][0:33, 0:384], True, True, ["rb33", ("scr", so)], [("ps", 1)])
            si = nscr()
            dve_copy(scr[si][0:4, 0:384], ps[1][0:4, 0:384], [("ps", 1)], [("scr", si)])
            P.dma("sp", fx_d[4 * g:4 * g + 4, :], scr[si][0:4, 0:384], reads=[("scr", si)], writes=["fx"])
        for g in (1, 2):
            so = nscr()
            P.dma("sp", scr[so][0:33, 0:384], oh_d[3], writes=[("scr", so)])
            mm(ps[1][0:4, 0:384], rb33[:, 4 * g:4 * g + 4], scr[so][0:33, 0:384], True, True, ["rb33", ("scr", so)], [("ps", 1)])
            si = nscr()
            dve_copy(scr[si][0:4, 0:384], ps[1][0:4, 0:384], [("ps", 1)], [("scr", si)])
            P.dma("sp", fx_d[8 + 4 * g:12 + 4 * g, :], scr[si][0:4, 0:384], reads=[("scr", si)], writes=["fx"])
        P.add("dve", lambda e: e.memset(onesf[:], 1.0), writes=["onesf"])
        si = nscr()
        for g in (1, 2):
            for h in range(4):
                jc = (g - 1) * 4 + h
                P.dma("sp", scr[si][:, jc:jc + 1], bass.AP(fx_d.tensor, (4 * g + h) * 384 + 128, [[1, 128], [1, 1]]),
                      reads=["fx"] + ([("scr", si)] if jc else []), writes=[("scr", si) if jc == 0 else ("scr", si, jc)])
        for h in range(4):
            P.dma("sp", scr[si][:, 8 + 8 * h:16 + 8 * h], bass.AP(fx_d.tensor, h * 384 + 128, [[1, 128], [1, 8]]),
                  reads=["fx", ("scr", si)], writes=[("scr", si, 8 + h)])
        b = genbank()
        mm(ps[b][:, 0:40], jflip[:, :], scr[si][:, 0:40], True, True, ["jflip", ("scr", si)] + [("scr", si, q) for q in range(1, 12)], [("ps", b)])
        dve_copy(sbias[:, :], ps[b][:, 0:40], [("ps", b)], ["sbias"])
        si = nscr()
        for g in range(3):
            for h in range(4):
                row = h if g == 0 else 12 + 4 * (g - 1) + h
                jc = g * 4 + h
                P.dma("sp", scr[si][0:8, 8 * jc:8 * jc + 8], bass.AP(fx_d.tensor, row * 384 + 120, [[1, 8], [1, 8]]),
                      reads=["fx"] + ([("scr", si)] if jc else []), writes=[("scr", si) if jc == 0 else ("scr", si, jc)])
        b = genbank()
        mm(ps[b][0:8, 0:96], jflip[0:8, 120:128], scr[si][0:8, 0:96], True, True, ["jflip", ("scr", si)] + [("scr", si, q) for q in range(1, 12)], [("ps", b)])
        dve_copy(sbiasN[:, :], ps[b][0:8, 0:96], [("ps", b)], ["sbiasN"])
        CK("s4")
        for g in range(3):
            for typ in range(2):
                si = nscr()
                for h in range(4):
                    hank = bass.AP(fx_d.tensor, (4 * g + h) * 384 + 128 * typ, [[1, 128], [1, 128]])
                    P.dma("sp", scr[si][:, 128 * h:128 * h + 128], hank, reads=["fx"] + ([("scr", si)] if h else []), writes=[("scr", si) if h == 0 else ("scr", si, h)])
                if g == 0 and typ == 0:
                    CK("s5")
                b = genbank()
                mm(ps[b][:, :], jflip[:, :], scr[si][:, :], True, True, ["jflip", ("scr", si)] + [("scr", si, h) for h in range(4)], [("ps", b)])
                evac(biasT[:, g, typ, :, :], ps[b][:, :].rearrange("p (h q) -> p h q", h=4), [("ps", b)], [("biasT", g, typ)])

    class Ctx:
        pass


    def norm_stats(cx):
        nr, ntb = cx.nr, cx.ntb
        for tb in range(ntb):
            act(junk[0:nr, :], xres[0][0:nr, tb, :], AF.Square, xk(tb), JK + [("ss", tb)], accum_out=stat[0:nr, tb:tb + 1])
            dve_ts(stat[0:nr, 8 + tb:9 + tb], stat[0:nr, tb:tb + 1], 1.0 / D, 1e-6, ALU.mult, ALU.add, [("ss", tb)], [("ms", tb)])
            P.add("pool", lambda e, o=stat[0:nr, 16 + tb:17 + tb], i0=stat[0:nr, 8 + tb:9 + tb], i1=chalf[0:nr, 0:1]: e.tensor_tensor(out=o, in0=i0, in1=i1, op=ALU.pow),
                  [("ms", tb), "chalf"], [("rstd", tb)])

    def norm_to_hT(cx, gcol):
        nr, ntb, N = cx.nr, cx.ntb, cx.N
        norm_stats(cx)
        CK("m2a")
        for tb in range(ntb):
            dve_ts(AT[0:nr, tb * D:(tb + 1) * D], xres[0][0:nr, tb, :], stat[0:nr, 16 + tb:17 + tb], None, ALU.mult, None,
                   xk(tb) + [("rstd", tb)], [("actT", 2 * tb), ("actT", 2 * tb + 1)])
        CK("m2b")
        for kc in range(8):
            b = genbank()
            for tb in range(ntb):
                mm(ps[b][:, tb * 128:tb * 128 + nr], AT[0:nr, tb * D + kc * 128:tb * D + (kc + 1) * 128], identb[0:nr, 0:nr], True, True,
                   [("actT", 2 * tb + kc // 4), "identb"], [("ps", b)])
            evac(hT[:, kc, 0:N], ps[b][:, 0:N], [("ps", b), "colv"], [("hT", kc)], scale=colv[:, gcol + kc:gcol + kc + 1])

    def ffn(cx, wgu, wd):
        nr, ntb, N = cx.nr, cx.ntb, cx.N
        for ci in range(11):
            s = wload([(wgu[:, 256 * ci:256 * ci + 256], 0, 8, 256, 128), (wgu[:, DFF + 256 * ci:DFF + 256 * ci + 256], 256, 8, 256, 128)])
            for p2 in range(2):
                c = 2 * ci + p2
                bg, bu = genbank(), genbank()
                for kc in range(8):
                    mm(ps[bg][:, 0:N], wsl[s][:, kc, p2 * 128:(p2 + 1) * 128], hT[:, kc, 0:N], kc == 0, kc == 7, wk(s) + [("hT", kc)], [("ps", bg)])
                for kc in range(8):
                    mm(ps[bu][:, 0:N], wsl[s][:, kc, 256 + p2 * 128:256 + (p2 + 1) * 128], hT[:, kc, 0:N], kc == 0, kc == 7, wk(s) + [("hT", kc)], [("ps", bu)])
                si = nscr(6, 8)
                act(scr[si][:, 0:N], ps[bg][:, 0:N], AF.Silu, [("ps", bg)], [("scr", si)])
                dve_tt(actc(c)[:, 0:N], scr[si][:, 0:N], ps[bu][:, 0:N], ALU.mult, [("scr", si), ("ps", bu)], [("actT", c)])
        for half in range(2):
            for g3 in range(3):
                kcs = list(range(8 * g3, min(8 * g3 + 8, 22)))
                s = wload([(wd[128 * kcs[0]:128 * (kcs[-1] + 1), 512 * half:512 * half + 512], 0, len(kcs), 512, 128)])
                for tb in range(ntb):
                    pb_ = (4 if half == 0 else 0) + tb
                    for i, kc in enumerate(kcs):
                        mm(ps[pb_][0:nr, :], actc(kc)[:, tb * 128:tb * 128 + nr], wsl[s][:, i, 0:512], kc == 0, kc == 21,
                           wk(s) + [("actT", kc)], [("ps", pb_)])
            for tb in range(ntb):
                pb_ = (4 if half == 0 else 0) + tb
                xs_ = xres[0][0:nr, tb, half * 512:(half + 1) * 512]
                dve_stt(xs_, ps[pb_][0:nr, :], 0.5, xs_, ALU.mult, ALU.add, [("ps", pb_), ("x", tb, half)], [("x", tb, half)])

    def proj_fm(cx, col0, ncols, consume):
        N = cx.N
        nch = ncols // 128
        done = 0
        while done < nch:
            n = min(3 if ncols % 384 == 0 else 4, nch - done)
            s = wload([(win_d[:, col0 + 128 * done:col0 + 128 * (done + n)], 0, 8, 128 * n, 128)])
            for oc in range(n):
                b = genbank()
                for kc in range(8):
                    mm(ps[b][:, 0:N], wsl[s][:, kc, oc * 128:(oc + 1) * 128], hT[:, kc, 0:N], kc == 0, kc == 7, wk(s) + [("hT", kc)], [("ps", b)])
                consume(done + oc, b)
            yield s, done, n
            done += n


    TIX = (12, 13, 14, 15, 20, 21)

    def tix_buf(cx, c):
        if cx.mode != "prompt":
            return actc(TIX[c])[:, 0:cx.N], [("actT", TIX[c])]
        if c < 2:
            return QM[:, 6 + c, 0:cx.N], [("QM", 6 + c)]
        if c < 4:
            return actc(18 + c)[:, 0:cx.N], [("actT", 18 + c)]
        v2f = V2s[:, :, :].rearrange("p r c -> p (r c)")
        return v2f[:, 512 * (c - 4):512 * (c - 4) + cx.N], [("V2s", 2 * (c - 4)), ("V2s", 2 * (c - 4) + 1)]

    def lru_abuf(cx, c):
        if cx.mode == "prompt":
            return XR[:, c, 3:3 + cx.N], ("XR", c)
        return scr[c][:, 0:cx.N], ("scr", c)

    def lru_p1_stages(cx, c):
        N, nseq, L = cx.N, cx.nseq, cx.L
        ixc, iti, ixb = 6 + c % 2, 4 + c % 2, 3
        ab, akey = lru_abuf(cx, c)
        ti_buf, ti_key = (scr[iti][:, 0:N], ("scr", iti)) if cx.mode == "prompt" else (pbb[c % 2][:, 0:N], ("pb", c % 2))
        xc = scr[ixc][:, 0:N]
        xc3 = xc.rearrange("p (s l) -> p s l", s=nseq)
        xrv = XR[:, c, 0:nseq * (3 + L)].rearrange("p (s l) -> p s l", s=nseq)
        bk = 3

        def s0():
            act(xc3, xrv[:, :, 3:3 + L], AF.Identity, [("XR", c)], [("scr", ixc)],
                scale=colv[:, R_CW + 18 + c:R_CW + 19 + c], bias=colv[:, R_CB + c:R_CB + c + 1])

        def s1():
            for jt in range(3):
                dve_stt(xc3, xrv[:, :, jt:jt + L], colv[:, R_CW + 6 * jt + c:R_CW + 6 * jt + c + 1], xc3, ALU.mult, ALU.add,
                        [("XR", c), ("scr", ixc), "colv"], [("scr", ixc)])
            if cx.mode == "prompt":
                act(XR[:, c, 0:3], XR[:, c, L:L + 3], AF.Copy, [("XR", c)], [("XR", c)])
            dve_copy(pbb[ixb][:, 0:N], xc, [("scr", ixc)], [("pb", ixb)])

        def s2():
            mm(ps[bk][:, 0:N], wa_bd[:, c, :], pbb[ixb][:, 0:N], True, True, ["wa_bd", ("pb", ixb)], [("ps", bk)])

        def s3():
            act(ab, ps[bk][:, 0:N], AF.Tanh, [("ps", bk), "dcol"], [akey], scale=0.5, bias=dcol[:, C_HBA + c:C_HBA + c + 1])

        def s4():
            mm(ps[bk][:, 0:N], wi_bd[:, c, :], pbb[ixb][:, 0:N], True, True, ["wi_bd", ("pb", ixb)], [("ps", bk)])

        def s5():
            act(ab, ab, AF.Exp, [akey, "dcol"], [akey],
                scale=dcol[:, C_M4SP + c:C_M4SP + c + 1], bias=dcol[:, C_M4SP + c:C_M4SP + c + 1])
            act(ti_buf, ps[bk][:, 0:N], AF.Tanh, [("ps", bk), "dcol"], [ti_key], scale=0.5, bias=dcol[:, C_HBI + c:C_HBI + c + 1])

        def s6():
            tb_, tk_ = tix_buf(cx, c)
            dve_stt(tb_, ti_buf, 1.0, xc, ALU.add, ALU.mult, [ti_key, ("scr", ixc)], tk_)

        return [s0, s1, s2, s3, s4, s5, s6]

    def lru_p1(cx, c):
        for f in lru_p1_stages(cx, c):
            f()

    def lru_p1_slots(cx, stagger=4):
        slots = {}
        offs = (0, 1, 3, 4, 5, 6, 7)
        for c in range(6):
            for si_, f in enumerate(lru_p1_stages(cx, c)):
                slots.setdefault(stagger * c + offs[si_], []).append(f)
        out = []
        for t in range(max(slots) + 1):
            fs = slots.get(t, [])
            out.append(lambda fs=fs: [f() for f in fs])
        return out

    def lru_p2(cx):
        N, nseq, L = cx.N, cx.nseq, cx.L
        for c in range(6):
            ab, akey = lru_abuf(cx, c)
            iy = 6 + c % 2
            yb = scr[iy][:, 0:N]
            dve_tt(yb, ab, ab, ALU.mult, [akey], [("scr", iy)])
            act(yb, yb, AF.Sqrt, [("scr", iy)], [("scr", iy)], scale=-1.0, bias=1.0)
            tb_, tk_ = tix_buf(cx, c)
            dve_stt(yb, yb, 0.5, tb_, ALU.mult, ALU.mult, [("scr", iy)] + tk_, [("scr", iy)])
            for sq in range(nseq):
                sl = slice(sq * L, (sq + 1) * L)
                P.add("dve", lambda e, o=scr[iy][:, sl], a=ab[:, sl], u=scr[iy][:, sl], i0=hst[:, c, sq:sq + 1]:
                      e.tensor_tensor_scan(out=o, data0=a, data1=u, initial=i0, op0=ALU.mult, op1=ALU.add),
                      [akey, ("scr", iy), ("hst", c)], [("scr", iy)])
            hv = scr[iy][:, 0:N].rearrange("p (s l) -> p s l", s=nseq)[:, :, L - 1]
            dve_copy(hst[:, c, 0:nseq], hv, [("scr", iy)], [("hst", c)])
            dve_tt(actc(6 + c)[:, 0:N], actc(c)[:, 0:N], scr[iy][:, 0:N], ALU.mult, [("actT", c), ("scr", iy)], [("actT", 6 + c)])

    def lru_p2a(cx):
        N = cx.N
        for c in range(6):
            ab, akey = lru_abuf(cx, c)
            act(scr[c][:, 0:N], ab, AF.Square, [akey], [("scr", c)])
            act(scr[c][:, 0:N], scr[c][:, 0:N], AF.Sqrt, [("scr", c)], [("scr", c)], scale=-1.0, bias=1.0)

    def lru_p2b(cx, c):
        N, nseq, L = cx.N, cx.nseq, cx.L
        ab, akey = lru_abuf(cx, c)
        yb = scr[c][:, 0:N]
        tb_, tk_ = tix_buf(cx, c)
        dve_stt(yb, yb, 0.5, tb_, ALU.mult, ALU.mult, [("scr", c)] + tk_, [("scr", c)])
        for sq in range(nseq):
            sl = slice(sq * L, (sq + 1) * L)
            P.add("dve", lambda e, o=scr[c][:, sl], a=ab[:, sl], u=scr[c][:, sl], i0=hst[:, c, sq:sq + 1]:
                  e.tensor_tensor_scan(out=o, data0=a, data1=u, initial=i0, op0=ALU.mult, op1=ALU.add),
                  [akey, ("scr", c), ("hst", c)], [("scr", c)])
        hv = scr[c][:, 0:N].rearrange("p (s l) -> p s l", s=nseq)[:, :, L - 1]
        dve_copy(hst[:, c, 0:nseq], hv, [("scr", c)], [("hst", c)])
        dve_tt(actc(6 + c)[:, 0:N], actc(c)[:, 0:N], scr[c][:, 0:N], ALU.mult, [("actT", c), ("scr", c)], [("actT", 6 + c)])

    def lru_phase(cx):
        for c in range(6):
            lru_p1(cx, c)
        lru_p2(cx)

    def mem_attn(cx, mk_of, mv_of, nqs, mk_keys=lambda bi, h: [("mkT", bi)]):
        for h in range(4):
            Ob, Lb = 4 + (h % 2), 6 + (h % 2)
            for (c0, ncol, bi) in nqs:
                for blk in range(2):
                    b = genbank()
                    mm(ps[b][:, c0:c0 + ncol], mk_of(bi)[:, h, blk * 128:(blk + 1) * 128], qmT[:, h, c0:c0 + ncol], True, True,
                       mk_keys(bi, h) + [("qmT", h)], [("ps", b)])
                    pi = nscr(0, 4)
                    act(pbb[pi][:, c0:c0 + ncol], ps[b][:, c0:c0 + ncol], AF.Exp, [("ps", b)], [("pb", pi)], scale=float(128 ** -0.5))
                    mm(ps[Ob][:, c0:c0 + ncol], mv_of(bi)[:, blk, h, :], pbb[pi][:, c0:c0 + ncol], blk == 0, blk == 1, [("mv", bi), ("pb", pi)], [("ps", Ob)])
                    mm(ps[Lb][:, c0:c0 + ncol], onesb[:, :], pbb[pi][:, c0:c0 + ncol], blk == 0, blk == 1, ["onesb", ("pb", pi)], [("ps", Lb)])
            N = cx.N
            si = nscr(4, 6)
            P.add("dve", lambda e, o=scr[si][:, 0:N], i=ps[Lb][:, 0:N]: e.reciprocal(out=o, in_=i), [("ps", Lb)], [("scr", si)])
            dve_tt(actc(12 + h)[:, 0:N], ps[Ob][:, 0:N], scr[si][:, 0:N], ALU.mult, [("ps", Ob), ("scr", si)], [("actT", 12 + h)])

    def merge_and_out(cx, hooks=()):
        nr, ntb, N = cx.nr, cx.ntb, cx.N
        hooks = list(hooks)
        for c4 in range(2):
            branches = (
                (GA0, C_HBG + 0, [(wao_d[:, 512 * c4:512 * c4 + 512], 0, 4, 512, 64)], [(h, 64, ("actT", 16 + h), lambda h=h: actc(16 + h)[0:64, 0:N]) for h in range(4)]),
                (GB0, C_HBG + 8, [(wro_d[:, 512 * c4:512 * c4 + 512], 0, 6, 512, 128)], [(k, 128, ("actT", 6 + k), lambda k=k: actc(6 + k)[:, 0:N]) for k in range(6)]),
                (GC0, C_HBG + 16, [(wmo_d[:, 512 * c4:512 * c4 + 512], 0, 4, 512, 128)], [(k, 128, ("actT", 12 + k), lambda k=k: actc(12 + k)[:, 0:N]) for k in range(4)]),
            )
            for bi, (gcol0, hb0, oparts, kparts) in enumerate(branches):
                s = wload([(win_d[:, gcol0 + 512 * c4:gcol0 + 512 * c4 + 512], 0, 8, 512, 128)])
                for oc in range(4):
                    b = genbank()
                    for kc in range(8):
                        mm(ps[b][:, 0:N], wsl[s][:, kc, oc * 128:(oc + 1) * 128], hT[:, kc, 0:N], kc == 0, kc == 7, wk(s) + [("hT", kc)], [("ps", b)])
                    act(pbb[oc][:, 0:N], ps[b][:, 0:N], AF.Tanh, [("ps", b), "dcol"], [("pb", oc)], scale=0.5,
                        bias=dcol[:, hb0 + 4 * c4 + oc:hb0 + 4 * c4 + oc + 1])
                    if hooks:
                        hooks.pop(0)()
                s2 = wload(oparts)
                for oc in range(4):
                    b = genbank()
                    nkp = len(kparts)
                    for (ki, kp, key, apf) in kparts:
                        mm(ps[b][:, 0:N], wsl[s2][0:kp, ki, oc * 128:(oc + 1) * 128], apf(), ki == 0, ki == nkp - 1, wk(s2) + [key], [("ps", b)])
                    if bi == 0:
                        dve_stt(scr[oc][:, 0:N], pbb[oc][:, 0:N], 1.0, ps[b][:, 0:N], ALU.add, ALU.mult, [("pb", oc), ("ps", b)], [("scr", oc)])
                        if hooks:
                            hooks.pop(0)()
                    else:
                        ti = 4 + (oc % 2)
                        dve_stt(scr[ti][:, 0:N], pbb[oc][:, 0:N], 1.0, ps[b][:, 0:N], ALU.add, ALU.mult, [("pb", oc), ("ps", b)], [("scr", ti)])
                        if bi == 1:
                            dve_tt(scr[oc][:, 0:N], scr[oc][:, 0:N], scr[ti][:, 0:N], ALU.add, [("scr", oc), ("scr", ti)], [("scr", oc)])
                        else:
                            dve_tt(QM[:, 4 * c4 + oc, 0:N], scr[oc][:, 0:N], scr[ti][:, 0:N], ALU.add, [("scr", oc), ("scr", ti)], [("QM", 4 * c4 + oc)])
        for half in range(2):
            s = wload([(wout_d[:, 512 * half:512 * half + 512], 0, 8, 512, 128)])
            for tb in range(ntb):
                pb_ = (4 if half == 0 else 0) + tb
                for kc in range(8):
                    mm(ps[pb_][0:nr, :], QM[:, kc, tb * 128:tb * 128 + nr], wsl[s][:, kc, 0:512], kc == 0, kc == 7, wk(s) + [("QM", kc)], [("ps", pb_)])
            for tb in range(ntb):
                pb_ = (4 if half == 0 else 0) + tb
                xs_ = xres[0][0:nr, tb, half * 512:(half + 1) * 512]
                dve_stt(xs_, ps[pb_][0:nr, :], 0.5, xs_, ALU.mult, ALU.add, [("ps", pb_), ("x", tb, half)], [("x", tb, half)])

    def final_norm_store(cx, ydst, after_tb=None):
        nr, ntb = cx.nr, cx.ntb
        norm_stats(cx)
        for tb in range(ntb):
            ys_ = ystage[tb % 2]
            dve_stt(ys_[0:nr, :], xres[0][0:nr, tb, :], stat[0:nr, 16 + tb:17 + tb], gfin[0:nr, :], ALU.mult, ALU.mult,
                    xk(tb) + [("rstd", tb), "gfin"], [("yst", tb % 2)])
            if cx.mode == "prompt":
                P.dma("sp", ydst[tb * 128:(tb + 1) * 128, :], ys_[:, :], reads=[("yst", tb % 2)])
            else:
                P.dma("sp", ydst, ys_[0:nr, :], reads=[("yst", tb % 2)])
            if after_tb is not None:
                after_tb(tb)

    def cache_shift_copies():
        for g in range(3):
            Lc = GROUPS[g][0]
            for (src, dst) in ((ck_d[g], sk_d[g]), (cv_d[g], sv_d[g])):
                for b in range(NSB):
                    npr = (Lc - ST) * 16
                    P.dma("sp", bass.AP(dst.tensor, b * Lc * 256, [[npr, 16], [1, npr]]),
                          bass.AP(src.tensor, b * Lc * 256 + ST * 256, [[npr, 16], [1, npr]]))

    def sample_phase():
        cx = Ctx(); cx.nr, cx.ntb, cx.N, cx.mode, cx.nseq, cx.L = NS, 1, NS, "sample", NSB, ST
        N = NS
        P.barrier()
        Kg = [k2[:, i, :].bitcast(F32).rearrange("p (t c) -> p t c", t=8) for i in range(2)]
        Vg = [V2[:, i, :, :].rearrange("p r c -> p (r c)").bitcast(F32).rearrange("p (t c) -> p t c", t=8) for i in range(2)]
        qbuf = k01[:, :, :, :].rearrange("p a b n -> p (a b n)").bitcast(F32).rearrange("p (t c) -> p t c", t=8)
        q_tm = V0[:, :, :, :].rearrange("p a b n -> p (a b n)").bitcast(F32)
        k_tm = V1[:, :, :, :].rearrange("p a b n -> p (a b n)").bitcast(F32)
        bflat = biasT[:, :, :, :, :].rearrange("p g t h q -> p (g t h q)")
        v_tm = bflat[:, 0:768]
        K0s, V0s = bflat[:, 1024:1280], bflat[:, 1280:1536]
        KN, VN = bflat[0:8, 1536:2304], bflat[0:8, 2304:3072]
        mkT_s = QM[:, :, :].rearrange("p c n -> p (c n)").rearrange("p (b h k) -> p b h k", b=NSB, h=4)
        mvb_s = k2[:, 0, :].rearrange("p (b k h e) -> p b k h e", b=NSB, k=2, h=4)

        si = nscr()
        P.dma("sp", scr[si][0:72, 0:128], sconv_d, writes=[("scr", si)])
        b = genbank()
        mm(ps[b][:, 0:72], scr[si][0:72, 0:128], identf[0:72, 0:72], True, True, [("scr", si), "identf"], [("ps", b)])
        xh = XR[:, :, 0:NSB * (3 + ST)].rearrange("p c (s l) -> p c s l", s=NSB)[:, :, :, 0:3]
        P.add("dve", lambda e, o=xh, i_=ps[b][:, 0:72].rearrange("p (s l c) -> p c s l", s=NSB, l=3): e.tensor_copy(out=o, in_=i_),
              [("ps", b)], [("XR", c) for c in range(6)])
        si = nscr()
        P.dma("sp", scr[si][0:24, 0:128], slru_d, writes=[("scr", si)])
        b2 = genbank()
        mm(ps[b2][:, 0:24], scr[si][0:24, 0:128], identf[0:24, 0:24], True, True, [("scr", si), "identf"], [("ps", b2)])
        P.add("dve", lambda e, i_=ps[b2][:, 0:24].rearrange("p (s c) -> p c s", s=NSB): e.tensor_copy(out=hst[:, :, :], in_=i_),
              [("ps", b2)], [("hst", c) for c in range(6)])

        P.add("dve", lambda e: e.memset(xres[0][:, 0, :], 0.0), writes=xk(0))
        P.dma("sp", xres[0][0:NS, 0, :], xs_d, writes=xk(0))
        norm_to_hT(cx, R_G1)
        ffn(cx, w1gu_d, w1d_d)
        norm_to_hT(cx, R_GM)
        for _ in proj_fm(cx, YR0, 768, lambda c, b: act(actc(c)[:, 0:N], ps[b][:, 0:N], AF.Gelu_apprx_tanh, [("ps", b)], [("actT", c)])):
            pass
        for (seg, dst, key, scl) in ((Q0, q_tm, "q_tm", 0.125), (K0, k_tm, "k_tm", None), (V0c, v_tm, "v_tm", None)):
            for ch in range(2):
                s = wload([(win_d[:, seg + 384 * ch:seg + 384 * ch + 384], 0, 8, 384, 128)])
                b = genbank()
                for kc in range(8):
                    mm(ps[b][0:NS, 0:384], hT[:, kc, 0:NS], wsl[s][:, kc, 0:384], kc == 0, kc == 7, wk(s) + [("hT", kc)], [("ps", b)])
                evac(dst[0:NS, 384 * ch:384 * ch + 384], ps[b][0:NS, 0:384], [("ps", b)], [(key, ch)], scale=scl)
        P.dma("sp", qs_d[:, :], q_tm[0:NS, 0:768], reads=[("q_tm", 0), ("q_tm", 1)], writes=["qs"])
        for g in range(3):
            Lc = GROUPS[g][0]
            for (src, dst, key) in ((k_tm, sk_d[g], "k_tm"), (v_tm, sv_d[g], "v_tm")):
                for b in range(NSB):
                    P.dma("sp", dst[b, Lc - ST:Lc, :], src[ST * b:ST * b + ST, 256 * g:256 * g + 256], reads=[(key, 0), (key, 1)])
        def xr_consume(c, b):
            o = XR[:, c, 0:NSB * (3 + ST)].rearrange("p (s l) -> p s l", s=NSB)[:, :, 3:3 + ST]
            evac(o, ps[b][:, 0:N].rearrange("p (s l) -> p s l", s=NSB), [("ps", b)], [("XR", c)])
        for (s, done, n) in proj_fm(cx, XR0, 768, xr_consume):
            b = genbank()
            for kc in range(8):
                mm(ps[b][0:NS, 0:384], hT[:, kc, 0:NS], wsl[s][:, kc, 0:384], kc == 0, kc == 7, wk(s) + [("hT", kc)], [("ps", b)])
            si = nscr()
            dve_copy(scr[si][0:NS, 0:384], ps[b][0:NS, 0:384], [("ps", b)], [("scr", si)])
            for bb in range(NSB):
                P.dma("sp", sconvo_d[bb, :, 128 * done:128 * done + 384], scr[si][ST * bb + 5:ST * bb + 8, 0:384], reads=[("scr", si)])
        for _ in proj_fm(cx, QM0, 512, lambda c, b: evac(qmT[:, c, 0:N], ps[b][:, 0:N], [("ps", b)], [("qmT", c)])):
            pass
        CK("s_front")

        OSb, LSb = 4, 5
        st8 = {"o": True, "l": True}
        S_i, P_i, SN_i, PN_i = 0, 1, 2, 3
        for bq in range(NSB):
            P.dma("sp", KN, k_tm[ST * bq:ST * bq + ST, 0:768], reads=[("k_tm", 0), ("k_tm", 1)], writes=["KN"])
            P.dma("sp", VN, v_tm[ST * bq:ST * bq + ST, 0:768], reads=[("v_tm", 0), ("v_tm", 1)], writes=["VN"])
            for g in range(3):
                Lc, dil = GROUPS[g]
                i = (3 * bq + g) % 2
                Kt, Vt = Kg[i], Vg[i]
                P.dma("sp", qbuf, bass.AP(qs_d.tensor, ST * bq * 768 + 256 * g, [[0, 128], [768, 8], [1, 256]]), reads=["qs"], writes=["qbuf"])
                if g == 0:
                    P.dma("sp", K0s, ck_d[0][bq], writes=["K0s"])
                    P.dma("sp", V0s, cv_d[0][bq], writes=["V0s"])
                    dve_tt(Kt, bc_mid(K0s, 8), qbuf, ALU.mult, ["K0s", "qbuf"], [("Kt", i)])
                else:
                    for (T_, src, tm, key) in ((Kt, ck_d[g], k_tm, "Kt"), (Vt, cv_d[g], v_tm, "Vt")):
                        if g == 1:
                            P.dma("sp", T_[0:112, :, :], bass.AP(src.tensor, bq * Lc * 256, [[4 * 256, 112], [256, 8], [1, 256]]), writes=[(key, i)])
                            P.dma("sp", T_[112:127, :, :], bass.AP(src.tensor, bq * Lc * 256 + 112 * 4 * 256, [[4 * 256, 15], [256, 8], [1, 256]]),
                                  reads=[(key, i)], writes=[(key, i, "c")])
                            P.dma("sp", T_[127:128, 0:4, :], bass.AP(src.tensor, bq * Lc * 256 + 508 * 256, [[256, 1], [256, 4], [1, 256]]),
                                  reads=[(key, i)], writes=[(key, i, "a")])
                            for t4 in range(4):
                                P.dma("sp", T_[127:128, 4 + t4, :], tm[ST * bq + t4:ST * bq + t4 + 1, 256:512],
                                      reads=[(key, i), ("k_tm", 0), ("k_tm", 1), ("v_tm", 0), ("v_tm", 1)], writes=[(key, i, "b", t4)])
                        else:
                            P.dma("sp", T_, src[bq].rearrange("(m r) c -> m r c", r=16)[:, 0:8, :], writes=[(key, i)])
                    dve_tt(Kt, Kt, qbuf, ALU.mult, [("Kt", i), ("Kt", i, "a"), ("Kt", i, "c")] + [("Kt", i, "b", t4) for t4 in range(4)] + ["qbuf"], [("Kt", i)])
                P.add("dve", lambda e, o=scr[S_i][:, 0:32], a=Kt.rearrange("p t (h e) -> p (t h) e", h=4): e.tensor_reduce(out=o, in_=a, axis=mybir.AxisListType.X, op=ALU.add),
                      [("Kt", i)], [("scr", S_i)])
                S3 = scr[S_i][:, 0:32].rearrange("p (t h) -> p t h", h=4)
                if g == 0:
                    bia = sbias[:, 8:40].rearrange("p (h t) -> p t h", t=8)
                else:
                    bia = bc_mid(sbias[:, 4 * (g - 1):4 * (g - 1) + 4], 8)
                dve_tt(S3, S3, bia, ALU.add, [("scr", S_i), "sbias"], [("scr", S_i)])
                act(scr[P_i][:, 0:32], scr[S_i][:, 0:32], AF.Exp, [("scr", S_i)], [("scr", P_i)])
                if g == 0:
                    for h in range(4):
                        mm(ps[OSb][0:64, 32 * bq + h:32 * bq + 32:4], V0s[:, 64 * h:64 * h + 64], scr[P_i][:, h:32:4], st8["o"], False,
                           ["V0s", ("scr", P_i)], [("ps", OSb)])
                        st8["o"] = False
                else:
                    for t in range(8):
                        for h in range(4):
                            cidx = 32 * bq + 4 * t + h
                            mm(ps[OSb][0:64, cidx:cidx + 1], Vt[:, t, 64 * h:64 * h + 64], scr[P_i][:, 4 * t + h:4 * t + h + 1], st8["o"], False,
                               [("Vt", i), ("Vt", i, "a"), ("Vt", i, "c")] + [("Vt", i, "b", t4) for t4 in range(4)] + [("scr", P_i)], [("ps", OSb)])
                            st8["o"] = False
                mm(ps[LSb][0:64, 32 * bq:32 * bq + 32], onesf[:, 0:64], scr[P_i][:, 0:32], st8["l"], False, ["onesf", ("scr", P_i)], [("ps", LSb)])
                st8["l"] = False
                q8 = bass.AP(qbuf.tensor, qbuf.offset, [[qbuf.ap[0][0], 8]] + [list(a) for a in qbuf.ap[1:]])
                dve_tt(q8, bc_mid(KN[:, 256 * g:256 * g + 256], 8), q8, ALU.mult, ["KN", "qbuf", ("Kt", i)], ["qbuf"])
                P.add("dve", lambda e, o=scr[SN_i][0:8, 0:32], a=q8.rearrange("p t (h e) -> p (t h) e", h=4): e.tensor_reduce(out=o, in_=a, axis=mybir.AxisListType.X, op=ALU.add),
                      ["qbuf"], [("scr", SN_i)])
                SN3 = scr[SN_i][0:8, 0:32].rearrange("p (t h) -> p t h", h=4)
                dve_tt(SN3, SN3, sbiasN[:, 32 * g:32 * g + 32].rearrange("p (h t) -> p t h", t=8), ALU.add, [("scr", SN_i), "sbiasN"], [("scr", SN_i)])
                act(scr[PN_i][0:8, 0:32], scr[SN_i][0:8, 0:32], AF.Exp, [("scr", SN_i)], [("scr", PN_i)])
                last = (bq == NSB - 1 and g == 2)
                for h in range(4):
                    mm(ps[OSb][0:64, 32 * bq + h:32 * bq + 32:4], VN[:, 256 * g + 64 * h:256 * g + 64 * h + 64], scr[PN_i][0:8, h:32:4], False, last and h == 3,
                       ["VN", ("scr", PN_i)], [("ps", OSb)])
                mm(ps[LSb][0:64, 32 * bq:32 * bq + 32], onesf[0:8, 0:64], scr[PN_i][0:8, 0:32], False, last, ["onesf", ("scr", PN_i)], [("ps", LSb)])
        si = nscr(4, 6)
        P.add("dve", lambda e, o=scr[si][0:64, 0:128], a=ps[LSb][0:64, 0:128]: e.reciprocal(out=o, in_=a), [("ps", LSb)], [("scr", si)])
        for h in range(4):
            dve_tt(actc(16 + h)[0:64, 0:N], ps[OSb][0:64, h:128:4], scr[si][0:64, h:128:4], ALU.mult, [("ps", OSb), ("scr", si)], [("actT", 16 + h)])
        CK("s_attn")

        lru_phase(cx)
        b = genbank()
        mm(ps[b][0:24, 0:128], hst[:, :, :].rearrange("p c s -> p (c s)"), identf[:, :], True, True, [("hst", c) for c in range(6)] + ["identf"], [("ps", b)])
        si = nscr()
        dve_copy(scr[si][0:24, 0:128], ps[b][0:24, 0:128], [("ps", b)], [("scr", si)])
        for c in range(6):
            P.dma("sp", slruo_d[c:NSB * 6:6, :], scr[si][NSB * c:NSB * c + NSB, 0:128], reads=[("scr", si)])
        CK("s_lru")

        for bq in range(NSB):
            kt = Vg[bq % 2].rearrange("p t c -> p (t c)")
            ktv = kt[:, 0:1024].rearrange("p (k c) -> p k c", k=2)
            P.dma("sp", ktv, cmk_d[bq].rearrange("(k p) c -> p k c", p=128), writes=[("Vt", bq % 2)])
            P.dma("pool", mvb_s[:, bq, :, :, :].rearrange("p k h e -> p k (h e)"), cmv_d[bq].rearrange("(k p) c -> p k c", p=128), writes=[("mv", 1 + bq), ("Kt", 0)], lanes=[f"smv{bq % 2}"])
            for h in range(4):
                b = genbank()
                for blk in range(2):
                    mm(ps[b][:, blk * 128:(blk + 1) * 128], ktv[:, blk, h * 128:(h + 1) * 128], identf[:, :], True, True, [("Vt", bq % 2), "identf"], [("ps", b)])
                evac(mkT_s[:, bq, h, :], ps[b][:, 0:256], [("ps", b)], [("QM", 2 * bq + h // 2)])
        mem_attn(cx, lambda bi: mkT_s[:, bi - 1, :, :], lambda bi: mvb_s[:, bi - 1, :, :, :], [(ST * bq, ST, 1 + bq) for bq in range(NSB)],
                 mk_keys=lambda bi, h: [("QM", 2 * (bi - 1) + h // 2)])
        CK("s_mem")
        merge_and_out(cx)
        CK("s_merge")
        norm_to_hT(cx, R_G2)
        ffn(cx, w2gu_d, w2d_d)
        CK("s_ffn2")
        final_norm_store(cx, ys_d[:, :])

    def prompt_memory():
        cx = Ctx(); cx.nr, cx.ntb, cx.N, cx.mode = 128, 2, 256, "prompt"
        P.dma("sp", xres[0][:, 0:2, :], mem_d.rearrange("(tb p) d -> p tb d", p=128), writes=xk(0) + xk(1))
        CK("m1")
        norm_to_hT(cx, R_GMEM)
        CK("m3")
        for half in range(2):
            s = wload([(wmkv_d[:, 512 * half:512 * half + 512], 0, 8, 512, 128)])
            for blk in range(2):
                b = genbank()
                for kc in range(8):
                    mm(ps[b][:, :], hT[:, kc, blk * 128:(blk + 1) * 128], wsl[s][:, kc, 0:512], kc == 0, kc == 7, wk(s) + [("hT", kc)], [("ps", b)])
                CK("m4")
                si = nscr()
                dve_copy(scr[si][:, :], ps[b][:, :], [("ps", b)], [("scr", si)])
                CK("m5")
                dst = (pmk_d if half == 0 else pmv_d)[blk * 128:(blk + 1) * 128, :]
                P.dma("sp", dst, scr[si][:, :], reads=[("scr", si)])
                if half == 1:
                    act(mvb[:, blk, :, :], ps[b][:, :].rearrange("p (h e) -> p h e", h=4), AF.Copy, [("ps", b)], [("mv", 0)])
            CK("m6")
            if half == 0:
                for h in range(4):
                    b = genbank()
                    for kc in range(8):
                        mm(ps[b][:, 0:256], wsl[s][:, kc, h * 128:(h + 1) * 128], hT[:, kc, 0:256], kc == 0, kc == 7, wk(s) + [("hT", kc)], [("ps", b)])
                    evac(mkT[:, h, :], ps[b][:, 0:256], [("ps", b)], [("mkT", 0)])
                CK("m7")

    def attention_prompt(j, hooks=()):
        s_, b_, slot = j % 4, j // 4, j % 2
        pslot = 1 - slot
        jobs = []

        def add_job(h, g, typ, blocks, npart, cmin, cmax, bias_ap, l_out, l_rhs_view):
            jobs.append(dict(h=h, g=g, typ=typ, blocks=blocks, npart=npart, cmin=cmin, cmax=cmax, bias_ap=bias_ap, l_out=l_out,
                             l_rhs_view=l_rhs_view, first=False, last=False))

        for h in range(4):
            hp, pr0 = h // 2, (h % 2) * 64
            prs = slice(pr0, pr0 + 64)
            Ob, Lb = 4 + (h % 2), 6 + (h % 2)
            n0 = len(jobs)
            c = hp
            blocks = []
            for qb in range(4):
                cs = slice(qb * 128, (qb + 1) * 128)
                blocks.append((k01[prs, c, slot, cs], QM[prs, c, cs], qb * 128, 128, V0[:, slot, qb, 64 * h:64 * h + 64], ps[Ob][0:64, cs],
                               [("k01", c, slot)], [("V0", slot)]))
            add_job(h, 0, 0, blocks, 128, 0, 512, bc_mid(biasT[:, 0, 0, h, :], 4), ps[Lb][0:64, 0:512], lambda a: a)
            blocks = []
            for qb in range(4):
                if qb == 0 and j == 0:
                    continue
                cs = slice(qb * 128, (qb + 1) * 128)
                if qb == 0:
                    kap, vap, kk, vk = k01[prs, c, pslot, 384:512], V0[:, pslot, 3, 64 * h:64 * h + 64], [("k01", c, pslot)], [("V0", pslot)]
                else:
                    ks = slice((qb - 1) * 128, qb * 128)
                    kap, vap, kk, vk = k01[prs, c, slot, ks], V0[:, slot, qb - 1, 64 * h:64 * h + 64], [("k01", c, slot)], [("V0", slot)]
                blocks.append((kap, QM[prs, c, cs], qb * 128, 128, vap, ps[Ob][0:64, cs], kk, vk))
            cmin = 128 if j == 0 else 0
            add_job(h, 0, 1, blocks, 128, cmin, 512, bc_mid(biasT[:, 0, 1, h, :], (512 - cmin) // 128), ps[Lb][0:64, cmin:512], lambda a: a)
            c = 2 + hp
            o4 = ps[Ob][0:64, :].rearrange("p (i r) -> p r i", r=4)
            l4 = ps[Lb][0:64, :].rearrange("p (i r) -> p r i", r=4)
            for typ in range(2):
                if typ == 1 and j == 0:
                    continue
                ksl = slot if typ == 0 else pslot
                blocks = []
                for r in range(4):
                    kap = k01[prs, c, ksl, :].rearrange("p (i r) -> p r i", r=4)[:, r, :]
                    qap = QM[prs, c, :].rearrange("p (i r) -> p r i", r=4)[:, r, :]
                    blocks.append((kap, qap, r * 128, 128, V1[:, ksl, r, 64 * h:64 * h + 64], o4[:, r, :], [("k01", c, ksl)], [("V1", ksl)]))
                add_job(h, 1, typ, blocks, 128, 0, 512, bc_mid(biasT[:, 1, typ, h, :], 4), l4, lambda a: a.rearrange("p (r i) -> p r i", r=4))
            c = 4 + hp
            o16 = ps[Ob][0:64, :].rearrange("p (a r) -> p r a", r=16)
            l16 = ps[Lb][0:64, :].rearrange("p (a r) -> p r a", r=16)
            npc = 32 * (s_ + 1)
            blocks = []
            for r in range(16):
                base = 2048 * b_ + r
                kap = k2[prs, hp, base:base + 16 * (npc - 1) + 1:16]
                qap = QM[prs, c, :].rearrange("p (a r) -> p r a", r=16)[:, r, :]
                blocks.append((kap, qap, r * 32, 32, V2[0:npc, b_, r, 64 * h:64 * h + 64], o16[:, r, :], [("k2", hp, t) for t in range(4 * b_, j + 1)], [("V2", b_)]))
            add_job(h, 2, 0, blocks, npc, 0, 512, bc_mid(biasT[0:npc, 2, 0, h, 32 * s_:32 * s_ + 32], 16), l16,
                    lambda a: a.rearrange("p (r a) -> p r a", r=16))
            if b_ >= 1:
                blocks = []
                for r in range(16):
                    base = 2048 * (b_ - 1) + r
                    kap = k2[prs, hp, base:base + 16 * 127 + 1:16]
                    qap = QM[prs, c, :].rearrange("p (a r) -> p r a", r=16)[:, r, :]
                    blocks.append((kap, qap, r * 32, 32, V2[:, b_ - 1, r, 64 * h:64 * h + 64], o16[:, r, :],
                                   [("k2", hp, t) for t in range(4 * (b_ - 1), 4 * b_)], [("V2", b_ - 1)]))
                add_job(h, 2, 1, blocks, 128, 0, 512, bc_mid(biasT[:, 2, 1, h, 32 * s_:32 * s_ + 32], 16), l16,
                        lambda a: a.rearrange("p (r a) -> p r a", r=16))
            jobs[n0]["first"] = True
            jobs[-1]["last"] = True

        rot = [0]

        def emit_scores(jb):
            h, g, typ, npart, cmin, cmax = jb["h"], jb["g"], jb["typ"], jb["npart"], jb["cmin"], jb["cmax"]
            hp = h // 2
            b = rot[0] % 3
            rot[0] += 1
            for (k_ap, q_ap, c0, nq, v_ap, o_ap, kkeys, vkeys) in jb["blocks"]:
                mm(ps[b][0:npart, c0:c0 + nq], k_ap, q_ap, True, True, kkeys + [("QM", 2 * g + hp)], [("ps", b)])
            si = rot[0] % 3
            P.add("dve", lambda e, o=scr[si][0:npart, cmin:cmax], i0=ps[b][0:npart, cmin:cmax], i1=jb["bias_ap"]:
                  e.tensor_tensor(out=o.rearrange("p (a q) -> p a q", a=i1.shape[1]), in0=i0.rearrange("p (a q) -> p a q", a=i1.shape[1]), in1=i1, op=ALU.add),
                  [("ps", b), ("biasT", g, typ)], [("scr", si)])
            pi = rot[0] % 3
            act(pbb[pi][0:npart, cmin:cmax], scr[si][0:npart, cmin:cmax], AF.Exp, [("scr", si)], [("pb", pi)])
            jb["pi"] = pi

        def emit_pv(jb):
            h, npart, cmin, cmax, pi = jb["h"], jb["npart"], jb["cmin"], jb["cmax"], jb["pi"]
            Ob, Lb = 4 + (h % 2), 6 + (h % 2)
            nblk = len(jb["blocks"])
            for bi2, (k_ap, q_ap, c0, nq, v_ap, o_ap, kkeys, vkeys) in enumerate(jb["blocks"]):
                mm(o_ap, v_ap, pbb[pi][0:npart, c0:c0 + nq], jb["first"] and bi2 == 0, jb["last"] and bi2 == nblk - 1, vkeys + [("pb", pi)], [("ps", Ob)])
            mm(jb["l_out"], onesb[0:npart, 0:64], jb["l_rhs_view"](pbb[pi][0:npart, cmin:cmax]), jb["first"], jb["last"], ["onesb", ("pb", pi)], [("ps", Lb)])
            if jb["last"]:
                si = 3
                P.add("dve", lambda e, o=scr[si][0:64, :], i=ps[Lb][0:64, :]: e.reciprocal(out=o, in_=i), [("ps", Lb)], [("scr", si)])
                dve_tt(actc(16 + h)[0:64, :], ps[Ob][0:64, :], scr[si][0:64, :], ALU.mult, [("ps", Ob), ("scr", si)], [("actT", 16 + h)])

        LOOK = 2
        hooks = list(hooks)
        for k in range(len(jobs) + LOOK):
            if k < len(jobs):
                emit_scores(jobs[k])
            if k - LOOK >= 0:
                emit_pv(jobs[k - LOOK])
            if hooks:
                hooks.pop(0)()
        while hooks:
            hooks.pop(0)()

    PRELOAD = [False]

    def prompt_tile(j):
        cx = Ctx(); cx.nr, cx.ntb, cx.N, cx.mode, cx.nseq, cx.L = 128, 4, TN, "prompt", 1, TN
        N = TN
        s_, b_, slot = j % 4, j // 4, j % 2
        def load_x(jj, tb):
            P.dma("sp", xres[0][:, tb, :], x_d[jj * TN + tb * 128:jj * TN + (tb + 1) * 128, :], writes=xk(tb))
        if j == 0 or not PRELOAD[0]:
            for tb in range(4):
                load_x(j, tb)
        if j == 0:
            cache_shift_copies()
        CK(f"t{j}_load")
        norm_to_hT(cx, R_G1)
        CK(f"t{j}_norm1")
        ffn(cx, w1gu_d, w1d_d)
        CK(f"t{j}_ffn1")
        if j == 0:
            setup_bias()
        norm_to_hT(cx, R_GM)
        for _ in proj_fm(cx, YR0, 768, lambda c, b: act(actc(c)[:, 0:N], ps[b][:, 0:N], AF.Gelu_apprx_tanh, [("ps", b)], [("actT", c)])):
            pass
        s = wload([(win_d[:, V0c:V0c + 256], 0, 8, 256, 128)])
        for tbp in range(2):
            b = genbank()
            for t2 in range(2):
                tb = 2 * tbp + t2
                for kc in range(8):
                    mm(ps[b][:, t2 * 256:(t2 + 1) * 256], hT[:, kc, tb * 128:(tb + 1) * 128], wsl[s][:, kc, 0:256], kc == 0, kc == 7, wk(s) + [("hT", kc)], [("ps", b)])
            evac(V0[:, slot, 2 * tbp:2 * tbp + 2, :], ps[b][:, :].rearrange("p (t c) -> p t c", t=2), [("ps", b)], [("V0", slot)])
            if j == NT - 1 and tbp == 1:
                si = nscr()
                dve_copy(scr[si][:, 0:256], ps[b][:, 256:512], [("ps", b)], [("scr", si)])
                P.dma("sp", pv_d[0][:, :], scr[si][:, 0:256], reads=[("scr", si)])
        s = wload([(win_d[:, V0c + 256:V0c + 768], 0, 8, 512, 128)])
        for r in range(4):
            b = genbank()
            for kc in range(8):
                lt = hT[:, kc, :].rearrange("p (i r) -> p r i", r=4)[:, r, :]
                mm(ps[b][:, :], lt, wsl[s][:, kc, 0:512], kc == 0, kc == 7, wk(s) + [("hT", kc)], [("ps", b)])
            evac(V1[:, slot, r, :], ps[b][:, 0:256], [("ps", b)], [("V1", slot)])
            evac(V2s[:, r, :], ps[b][:, 256:512], [("ps", b)], [("V2s", r)])
            if j >= 4:
                si = nscr()
                dve_copy(scr[si][:, :], ps[b][:, :], [("ps", b)], [("scr", si)])
                if j == NT - 1:
                    P.dma("sp", pv_d[1][r::4, :], scr[si][:, 0:256], reads=[("scr", si)])
                P.dma("sp", pv_d[2][(j - 4) * TN + r:(j - 3) * TN:4, :], scr[si][:, 256:512], reads=[("scr", si)])
        for q4 in range(4):
            P.dma("sp", V2[32 * s_:32 * s_ + 32, b_, 4 * q4:4 * q4 + 4, :], V2s[q4::4, :, :], reads=[("V2s", r) for r in range(4)], writes=[("V2", b_)])
        def k_consume(c, b):
            if c < 4:
                evac(k01[:, c, slot, :], ps[b][:, 0:N], [("ps", b)], [("k01", c, slot)])
            else:
                evac(k2[:, c - 4, j * TN:(j + 1) * TN], ps[b][:, 0:N], [("ps", b)], [("k2", c - 4, j)])
        for (s, done, n) in proj_fm(cx, K0, 768, k_consume):
            if j >= 4:
                ch = done // 3
                for tb in range(4):
                    b = genbank()
                    for kc in range(8):
                        mm(ps[b][:, 0:384], hT[:, kc, tb * 128:(tb + 1) * 128], wsl[s][:, kc, 0:384], kc == 0, kc == 7, wk(s) + [("hT", kc)], [("ps", b)])
                    si = nscr()
                    dve_copy(scr[si][:, 0:384], ps[b][:, 0:384], [("ps", b)], [("scr", si)])
                    if ch == 0:
                        if j == NT - 1:
                            if tb == 3:
                                P.dma("sp", pk_d[0][:, :], scr[si][:, 0:256], reads=[("scr", si)])
                            P.dma("sp", pk_d[1][tb * 128:(tb + 1) * 128, 0:128], scr[si][:, 256:384], reads=[("scr", si)])
                    else:
                        if j == NT - 1:
                            P.dma("sp", pk_d[1][tb * 128:(tb + 1) * 128, 128:256], scr[si][:, 0:128], reads=[("scr", si)])
                        P.dma("sp", pk_d[2][(j - 4) * TN + tb * 128:(j - 4) * TN + (tb + 1) * 128, :], scr[si][:, 128:384], reads=[("scr", si)])
        for _ in proj_fm(cx, Q0, 768, lambda c, b: act(QM[:, c, 0:N], ps[b][:, 0:N], AF.Copy, [("ps", b)], [("QM", c)], scale=0.125)):
            pass
        for (s, done, n) in proj_fm(cx, XR0, 768, lambda c, b: evac(XR[:, c, 3:3 + N], ps[b][:, 0:N], [("ps", b)], [("XR", c)])):
            if j == NT - 1:
                b = genbank()
                for kc in range(8):
                    mm(ps[b][0:3, 0:384], hT[:, kc, N - 3:N], wsl[s][:, kc, 0:384], kc == 0, kc == 7, wk(s) + [("hT", kc)], [("ps", b)])
                si = nscr()
                dve_copy(scr[si][0:3, 0:384], ps[b][0:3, 0:384], [("ps", b)], [("scr", si)])
                P.dma("sp", pconv_d[:, 128 * done:128 * done + 384], scr[si][0:3, 0:384], reads=[("scr", si)])
        for _ in proj_fm(cx, QM0, 512, lambda c, b: evac(qmT[:, c, 0:N], ps[b][:, 0:N], [("ps", b)], [("qmT", c)])):
            pass
        CK(f"t{j}_win")
        attention_prompt(j, hooks=lru_p1_slots(cx))
        CK(f"t{j}_attn")
        mem_attn(cx, lambda bi: mkT, lambda bi: mvb, [(0, N, 0)])
        CK(f"t{j}_memattn")
        lru_p2a(cx)
        CK(f"t{j}_lru")
        merge_and_out(cx, hooks=[(lambda c=c: lru_p2b(cx, c)) for c in range(6)])
        if j == NT - 1:
            b = genbank()
            mm(ps[b][0:6, 0:128], hst[:, :, 0], identf[:, :], True, True, [("hst", c) for c in range(6)] + ["identf"], [("ps", b)])
            si = nscr()
            dve_copy(scr[si][0:6, 0:128], ps[b][0:6, 0:128], [("ps", b)], [("scr", si)])
            P.dma("sp", plru_d[:, :], scr[si][0:6, 0:128], reads=[("scr", si)])
        CK(f"t{j}_merge")
        norm_to_hT(cx, R_G2)
        ffn(cx, w2gu_d, w2d_d)
        nxt = (lambda tb: load_x(j + 1, tb)) if (j + 1 < ntiles) else None
        PRELOAD[0] = nxt is not None
        final_norm_store(cx, y_d[j * TN:(j + 1) * TN, :], after_tb=nxt)

    try:
        setup()
        CK("setup")
        if ntiles == 0:
            cache_shift_copies()
        prompt_memory()
        CK("mem")
        if ntiles == 0:
            setup_bias()
        for j in range(ntiles):
            prompt_tile(j)
            CK(f"t{j}")
        CK("prompt")
        sample_phase()
    except _Stop:
        pass
    if upto is not None:
        dbg_x = dout("dbg_x", [512, D]); dbg_at = dout("dbg_at", [128, 22 * 512]); dbg_qm = dout("dbg_qm", [128, 4096]); dbg_ht = dout("dbg_ht", [128, 4096])
        P.dma("sp", dbg_x.rearrange("(tb p) d -> p tb d", p=128), xres[0][:, :, :], reads=[k for t in range(4) for k in xk(t)])
        dbg_bias = dout("dbg_bias", [128, 3072])
        P.dma("sp", dbg_bias, biasT[:, :, :, :, :].rearrange("p g t h q -> p (g t h q)"), reads=[("biasT", g, t) for g in range(3) for t in range(2)])
        P.dma("pool", dbg_at, AT[:, :], reads=[("actT", c) for c in range(22)], lanes=["dbg0"])
        P.dma("pool", dbg_qm, QM[:, :, :].rearrange("p c n -> p (c n)"), reads=[("QM", c) for c in range(8)], lanes=["dbg1"])
        P.dma("pool", dbg_ht, hT[:, :, :].rearrange("p c n -> p (c n)"), reads=[("hT", c) for c in range(8)], lanes=["dbg2"])

    P.emit(st)
    return nc, P, st


_CACHE = {}


def _get_program():
    if "nc" not in _CACHE:
        nc, P, st = build_program()
        _CACHE["nc"] = nc
        _CACHE["st"] = st
        _CACHE["P"] = P
    return _CACHE["nc"]


def make_in_maps(inp, cores=range(8)):
    f = lambda a: np.ascontiguousarray(np.asarray(a, dtype=np.float32))
    vec = lambda a: f(a).reshape(-1, 128)
    vecrows = np.concatenate([
        vec(inp["ffn1_norm"][0]), vec(inp["mix_norm"][0]), vec(inp["ffn2_norm"][0]), vec(inp["mem_norm"][0]),
        vec(inp["b_gate"][0]), vec(inp["conv_w"][0]), vec(inp["conv_b"][0]), vec(inp["lru_b_a"][0]),
        vec(inp["lru_b_i"][0]), vec(inp["lru_lambda"][0])], axis=0)
    assert vecrows.shape == (NROWS, 128)
    shared = {
        "rel_bias": f(inp["rel_bias"]), "vecrows": f(vecrows), "final_norm": f(inp["final_norm"]).reshape(1, D),
        "ffn1_w_gu": f(inp["ffn1_w_gu"][0]), "ffn1_w_down": f(inp["ffn1_w_down"][0]), "w_in": f(inp["w_in"][0]),
        "lru_w_a": f(inp["lru_w_a"][0]), "lru_w_i": f(inp["lru_w_i"][0]), "w_mem_kv": f(inp["w_mem_kv"][0]),
        "w_att_o": f(inp["w_att_o"][0]), "w_rec_o": f(inp["w_rec_o"][0]), "w_mem_o": f(inp["w_mem_o"][0]),
        "w_out": f(inp["w_out"][0]), "ffn2_w_gu": f(inp["ffn2_w_gu"][0]), "ffn2_w_down": f(inp["ffn2_w_down"][0]),
        "identf": np.eye(128, dtype=np.float32), "jflip": np.ascontiguousarray(np.eye(128, dtype=np.float32)[::-1]),
        "oh": _onehot_const(),
    }
    in_maps = []
    for c in cores:
        bs = slice(NSB * c, NSB * (c + 1))
        m = dict(shared)
        m["x"] = f(inp["x_prompt"][c]); m["xs"] = f(inp["x_sample"][bs]).reshape(NS, D); m["mem"] = f(inp["mem_prompt"][c])
        for g in range(3):
            m[f"ck{g}"] = f(inp[f"cache_win_k{g}"][0, bs]).reshape(NSB, -1, 256)
            m[f"cv{g}"] = f(inp[f"cache_win_v{g}"][0, bs]).reshape(NSB, -1, 256)
        m["sconv"] = f(inp["state_conv"][0, bs]).reshape(NSB * 3 * 6, 128)
        m["slru"] = f(inp["state_lru"][0, bs]).reshape(NSB * 6, 128)
        m["cmk"] = f(inp["cache_mem_k"][0, bs]).reshape(NSB, 256, 512)
        m["cmv"] = f(inp["cache_mem_v"][0, bs]).reshape(NSB, 256, 512)
        in_maps.append(m)
    return in_maps


def kernel(**inp):
    nc = _get_program()
    n = 8
    in_maps = make_in_maps(inp)
    res = run_bass_kernel_spmd(nc, in_maps, core_ids=list(range(n)))
    R = res.results
    cat = lambda k: np.stack([np.asarray(R[c][k], dtype=np.float32) for c in range(n)], axis=0)
    y_prompt = cat("y")
    y_sample = cat("ys").reshape(32, 8, D)
    outs = [y_prompt, y_sample]
    for g in range(3):
        L = min(GROUPS[g][0], SEQ)
        outs.append(cat(f"pk{g}").reshape(1, 8, L, 4, 64))
        outs.append(cat(f"pv{g}").reshape(1, 8, L, 4, 64))
    outs.append(cat("pconv").reshape(1, 8, 3, 768))
    outs.append(cat("plru").reshape(1, 8, 768))
    outs.append(cat("pmk").reshape(1, 8, 256, 4, 128))
    outs.append(cat("pmv").reshape(1, 8, 256, 4, 128))
    for g in range(3):
        L = GROUPS[g][0]
        outs.append(cat(f"sk{g}").reshape(1, 32, L, 4, 64))
        outs.append(cat(f"sv{g}").reshape(1, 32, L, 4, 64))
    outs.append(cat("sconvo").reshape(1, 32, 3, 768))
    outs.append(cat("slruo").reshape(1, 32, 768))
    return tuple(outs)
```

```python
import contextlib
import sys
import numpy as np
import concourse.bass as bass
import concourse.mybir as mybir
from concourse.bass_utils import run_bass_kernel_spmd

F32 = mybir.dt.float32
BF16 = mybir.dt.bfloat16
AF = mybir.ActivationFunctionType
ALU = mybir.AluOpType

PROCS = ("pe", "act", "dve", "pool", "sp")
SAME_ENGINE_SYNC = True


class _Op:
    __slots__ = ("proc", "fn", "seq", "signal", "sigidx", "lane", "lanecnt", "waits", "snap", "is_dma", "where")


class Prog:
    def __init__(self, nc):
        self.nc = nc
        self.ops = {p: [] for p in PROCS}
        self.last_w = {}
        self.readers = {}
        self.clock = {p: {} for p in PROCS}
        self.lane_last = {}
        self.lane_proc = {}
        self.rr = {}
        self.pending = {}

    def barrier(self):
        for p in PROCS:
            lst = [self.ops[q][-1] for q in PROCS if q != p and self.ops[q] and not self.ops[q][-1].is_dma]
            for q in PROCS:
                if q != p:
                    for o in reversed(self.ops[q]):
                        if not o.is_dma:
                            lst.append(o)
                            break
            lst += list(self.lane_last.values())
            self.pending[p] = lst

    def _dep(self, op, y):
        p = op.proc
        if y.is_dma:
            src, val = y.lane, y.lanecnt
        else:
            src, val = y.proc, y.seq
            if y.proc == p and (p == "pe" or not SAME_ENGINE_SYNC):
                return
        ck = self.clock[p]
        if ck.get(src, -1) >= val:
            return
        if not y.is_dma:
            y.signal = True
        op.waits.append((src, val, y))
        ck[src] = val
        for s, v in y.snap.items():
            if ck.get(s, -1) < v:
                ck[s] = v

    def add(self, proc, fn, reads=(), writes=(), lane=None):
        op = _Op()
        op.proc = proc
        op.fn = fn
        op.seq = len(self.ops[proc])
        op.signal = False
        op.sigidx = None
        op.lane = lane
        op.lanecnt = None
        op.waits = []
        op.is_dma = lane is not None
        f_ = sys._getframe(1)
        w_ = []
        while f_ is not None and len(w_) < 4:
            w_.append(f_.f_lineno)
            f_ = f_.f_back
        op.where = w_
        writes = list(writes) + [k for k in reads if isinstance(k, tuple) and k[0] == "ps" and k not in writes]
        deps = []
        for k in reads:
            w = self.last_w.get(k)
            if w is not None:
                deps.append(w)
        for k in writes:
            w = self.last_w.get(k)
            if w is not None:
                deps.append(w)
            deps.extend(self.readers.get(k, ()))
        deps.extend(self.pending.pop(proc, ()))
        if lane is not None:
            prev = self.lane_last.get(lane)
            if prev is not None:
                deps.append(prev)
                op.lanecnt = prev.lanecnt + 1
            else:
                op.lanecnt = 1
            assert self.lane_proc.setdefault(lane, proc) == proc
        seen = set()
        for y in sorted(deps, key=lambda o: -o.seq):
            if id(y) in seen:
                continue
            seen.add(id(y))
            self._dep(op, y)
        op.snap = dict(self.clock[proc])
        self.ops[proc].append(op)
        if lane is not None:
            self.lane_last[lane] = op
        for k in reads:
            self.readers.setdefault(k, []).append(op)
        for k in writes:
            self.last_w[k] = op
            self.readers[k] = []
        return op

    def dma(self, proc, out, in_, reads=(), writes=(), lanes=None, **kw):
        if lanes is None:
            lanes = [f"{proc}{i}" for i in range(8)]
        key = tuple(lanes)
        i = self.rr.get(key, 0)
        self.rr[key] = i + 1
        lane = lanes[i % len(lanes)]
        return self.add(proc, lambda e: e.dma_start(out=out, in_=in_, **kw), reads, writes, lane=lane)

    def emit(self, st):
        nc = self.nc
        fin = _Op()
        fin.proc = "sp"; fin.fn = None; fin.seq = len(self.ops["sp"]); fin.signal = False
        fin.sigidx = None; fin.lane = None; fin.lanecnt = None; fin.waits = []; fin.is_dma = False; fin.where = []
        for lane, y in self.lane_last.items():
            self._dep(fin, y)
        for p in PROCS:
            if p != "sp" and self.ops[p]:
                self._dep(fin, self.ops[p][-1])
        fin.snap = {}
        self.ops["sp"].append(fin)
        for p in PROCS:
            n = 0
            for op in self.ops[p]:
                if op.signal and not op.is_dma:
                    n += 1
                    op.sigidx = n
        sems = {}
        for p in PROCS:
            sems[p] = st.enter_context(nc.semaphore(f"s_{p}"))
        for lane in self.lane_last:
            sems[lane] = st.enter_context(nc.semaphore(f"l_{lane}"))
        block = st.enter_context(nc.Block())

        def run(proc):
            def body(eng):
                for op in self.ops[proc]:
                    for (src, val, y) in op.waits:
                        if y.is_dma:
                            eng.wait_ge(sems[src], 16 * val)
                        else:
                            eng.wait_ge(sems[src], y.sigidx)
                    if op.fn is None:
                        continue
                    try:
                        ins = op.fn(eng)
                    except Exception:
                        print("EMIT FAILED for op at lines", op.where, "proc", op.proc)
                        raise
                    if op.is_dma:
                        ins.then_inc(sems[op.lane], 16)
                    elif op.signal:
                        ins.then_inc(sems[proc], 1)
            return body

        block.tensor(run("pe"))
        block.scalar(run("act"))
        block.vector(run("dve"))
        block.gpsimd(run("pool"))
        block.sync(run("sp"))

    def stats(self):
        return {p: (len(self.ops[p]), sum(1 for o in self.ops[p] if o.signal), sum(len(o.waits) for o in self.ops[p])) for p in PROCS}


D = 1024
DFF = 2816
SEQ = 4096
TN = 512
NT = SEQ // TN
NSB = 4
ST = 8
NS = NSB * ST
NEG = -30000.0
NW = 4
Q0, K0, V0c, XR0, YR0, QM0, GA0, GB0, GC0 = 0, 768, 1536, 2304, 3072, 3840, 4352, 5376, 6400
R_G1, R_GM, R_G2, R_GMEM, R_BG, R_CW, R_CB, R_BA, R_BI, R_LAM, NROWS = 0, 8, 16, 24, 32, 56, 80, 86, 92, 98, 104
GROUPS = ((128, 1), (512, 4), (2048, 16))


def _onehot_const():
    oh = np.zeros((4, 33, 384), np.float32)
    oh[3, 32, :] = NEG
    oh[3, 0, 127] = 1.0
    oh[3, 32, 127] = 0.0
    for g, (_, d) in enumerate(GROUPS):
        n = np.arange(129, dtype=np.int64) * d
        nf = np.maximum(n, 1).astype(np.float32)
        large = 16 + (np.log(nf / np.float32(16)) / np.float32(np.log(128.0)) * np.float32(16)).astype(np.int32)
        large = np.minimum(large, 31)
        bk = np.where(n < 16, n, large)
        oh[g, 32, :] = NEG
        for delta in range(129):
            oh[g, int(bk[delta]), 127 + delta] = 1.0
            oh[g, 32, 127 + delta] = 0.0
    return oh


class _Stop(Exception):
    pass


def build_program(upto=None, ntiles=NT):
    nc = bass.Bass("TRN2", target_bir_lowering=False)
    P = Prog(nc)
    st = contextlib.ExitStack()

    def din(name, shape):
        return nc.dram_tensor(name, list(shape), F32, kind="ExternalInput").ap()

    def dout(name, shape):
        return nc.dram_tensor(name, list(shape), F32, kind="ExternalOutput").ap()

    def sb(name, shape, dt):
        return st.enter_context(nc.sbuf_tensor("sb_" + name, list(shape), dt))

    x_d = din("x", [SEQ, D]); xs_d = din("xs", [NS, D]); mem_d = din("mem", [256, D])
    ck_d = [din(f"ck{g}", [NSB, GROUPS[g][0], 256]) for g in range(3)]
    cv_d = [din(f"cv{g}", [NSB, GROUPS[g][0], 256]) for g in range(3)]
    sconv_d = din("sconv", [NSB * 3 * 6, 128]); slru_d = din("slru", [NSB * 6, 128])
    cmk_d = din("cmk", [NSB, 256, 512]); cmv_d = din("cmv", [NSB, 256, 512])
    relb_d = din("rel_bias", [32, 12]); vrows_d = din("vecrows", [NROWS, 128]); gfin_d = din("final_norm", [1, D])
    w1gu_d = din("ffn1_w_gu", [D, 2 * DFF]); w1d_d = din("ffn1_w_down", [DFF, D])
    win_d = din("w_in", [D, 7424])
    lwa_d = din("lru_w_a", [12, 64, 64]); lwi_d = din("lru_w_i", [12, 64, 64])
    wmkv_d = din("w_mem_kv", [D, D]); wao_d = din("w_att_o", [256, D]); wro_d = din("w_rec_o", [768, D])
    wmo_d = din("w_mem_o", [512, D]); wout_d = din("w_out", [D, D])
    w2gu_d = din("ffn2_w_gu", [D, 2 * DFF]); w2d_d = din("ffn2_w_down", [DFF, D])
    identf_d = din("identf", [128, 128]); jflip_d = din("jflip", [128, 128]); oh_d = din("oh", [4, 33, 384])

    y_d = dout("y", [SEQ, D]); ys_d = dout("ys", [NS, D])
    pk_d = [dout(f"pk{g}", [min(GROUPS[g][0], SEQ), 256]) for g in range(3)]
    pv_d = [dout(f"pv{g}", [min(GROUPS[g][0], SEQ), 256]) for g in range(3)]
    pconv_d = dout("pconv", [3, 768]); plru_d = dout("plru", [6, 128])
    pmk_d = dout("pmk", [256, 512]); pmv_d = dout("pmv", [256, 512])
    sk_d = [dout(f"sk{g}", [NSB, GROUPS[g][0], 256]) for g in range(3)]
    sv_d = [dout(f"sv{g}", [NSB, GROUPS[g][0], 256]) for g in range(3)]
    sconvo_d = dout("sconvo", [NSB, 3, 768]); slruo_d = dout("slruo", [NSB * 6, 128])
    fx_d = nc.dram_tensor("fx_scratch", [20, 384], F32, kind="Internal").ap()
    qs_d = nc.dram_tensor("qs_scratch", [NS, 768], F32, kind="Internal").ap()

    wsl = [sb(f"wsl{i}", [128, 8, 512], BF16) for i in range(NW)]
    xres = [sb("xres0", [128, 4, D], F32)]
    AT = sb("actT", [128, 22 * 512], BF16)
    junk = AT[:, 20 * 512:22 * 512]
    JK = [("actT", 20), ("actT", 21)]
    ystage = [sb(f"ystage{i}", [128, D], F32) for i in range(2)]
    hT = sb("hT", [128, 8, 512], BF16)
    QM = sb("QM", [128, 8, 512], BF16)
    k01 = sb("k01", [128, 4, 2, 512], BF16)
    k2 = sb("k2", [128, 2, SEQ], BF16)
    V0 = sb("V0", [128, 2, 4, 256], BF16)
    V1 = sb("V1", [128, 2, 4, 256], BF16)
    V2 = sb("V2", [128, 2, 16, 256], BF16)
    V2s = sb("V2s", [128, 4, 256], BF16)
    biasT = sb("biasT", [128, 3, 2, 4, 128], F32)
    scr = [sb(f"scr{i}", [128, 512], F32) for i in range(8)]
    pbb = [sb(f"pb{i}", [128, 512], BF16) for i in range(4)]
    XR = sb("XR", [128, 6, 3 + 512], F32)
    qmT = sb("qmT", [128, 4, 512], BF16)
    mkT = sb("mkT", [128, 4, 256], BF16)
    mvb = sb("mvb", [128, 2, 4, 128], BF16)
    gfin = sb("gfin", [128, D], F32)
    colv = sb("colv", [128, NROWS], F32)
    dcol = sb("dcol", [128, 64], F32)
    vrows = sb("vrows", [NROWS, 128], F32)
    identf = sb("identf", [128, 128], F32)
    identb = sb("identb", [128, 128], BF16)
    jflip = sb("jflipb", [128, 128], F32)
    onesb = sb("onesb", [128, 128], BF16)
    wa_bd = sb("wa_bd", [128, 6, 128], BF16)
    wi_bd = sb("wi_bd", [128, 6, 128], BF16)
    stat = sb("stat", [128, 32], F32)
    chalf = sb("chalf", [128, 8], F32)
    hst = sb("hst", [128, 6, NSB], F32)
    rb33 = sb("rb33", [33, 12], F32)
    sbias = sb("sbias", [128, 40], F32)
    sbiasN = sb("sbiasN", [8, 96], F32)
    onesf = sb("onesf", [128, 64], F32)
    ps = [st.enter_context(nc.psum_tensor(f"ps{i}", [128, 512], F32)) for i in range(8)]

    def actc(c):
        return AT[:, c * 512:(c + 1) * 512]

    def bc_mid(ap2, n):
        (p0, pn), (s1, m) = ap2.ap
        return bass.AP(ap2.tensor, ap2.offset, [[p0, pn], [0, n], [s1, m]])

    gen_i = [0]

    def genbank():
        b = gen_i[0] % 4
        gen_i[0] += 1
        return b

    scr_i = [0]

    def nscr(lo=0, hi=8):
        i = lo + scr_i[0] % (hi - lo)
        scr_i[0] += 1
        return i

    def mm(out, lhsT, rhs, start, stop, reads, writes):
        P.add("pe", lambda e: e.matmul(out, lhsT, rhs, start=start, stop=stop, skip_group_check=True), reads, writes)

    def act(out, in_, func, reads, writes, **kw):
        P.add("act", lambda e: e.activation(out=out, in_=in_, func=func, **kw), reads, writes)

    def dve_ts(out, in0, s1, s2, op0, op1, reads, writes):
        if op1 is None:
            P.add("dve", lambda e: e.tensor_scalar(out=out, in0=in0, scalar1=s1, scalar2=None, op0=op0), reads, writes)
        else:
            P.add("dve", lambda e: e.tensor_scalar(out=out, in0=in0, scalar1=s1, scalar2=s2, op0=op0, op1=op1), reads, writes)

    def dve_stt(out, in0, scalar, in1, op0, op1, reads, writes):
        P.add("dve", lambda e: e.scalar_tensor_tensor(out=out, in0=in0, scalar=scalar, in1=in1, op0=op0, op1=op1), reads, writes)

    def dve_tt(out, in0, in1, op, reads, writes):
        P.add("dve", lambda e: e.tensor_tensor(out=out, in0=in0, in1=in1, op=op), reads, writes)

    def dve_copy(out, in_, reads, writes):
        P.add("dve", lambda e: e.tensor_copy(out=out, in_=in_), reads, writes)

    ev_i = [0]

    def evac(out, in_, reads, writes, scale=None):
        ev_i[0] += 1
        if ev_i[0] % 2 == 0:
            if scale is None:
                act(out, in_, AF.Copy, reads, writes)
            else:
                act(out, in_, AF.Copy, reads, writes, scale=scale)
        else:
            if scale is None:
                dve_copy(out, in_, reads, writes)
            else:
                dve_ts(out, in_, scale, None, ALU.mult, None, reads, writes)

    widx = [0]

    def wload(parts):
        s = widx[0] % NW
        widx[0] += 1
        for pi, (src, off, nk, ncol, kp) in enumerate(parts):
            dst = wsl[s][0:kp, 0:nk, off:off + ncol]
            P.dma("pool", dst, src.rearrange("(k p) c -> p k c", p=kp), writes=[("w", s, pi)], lanes=[f"w{s}_{pi}"])
        return s

    def wk(s):
        return [("w", s, 0), ("w", s, 1)]

    def xk(tb):
        return [("x", tb, 0), ("x", tb, 1)]

    C_HBG, C_HBA, C_HBI, C_M4SP = 0, 24, 30, 36

    P.marks = []

    def CK(name):
        P.marks.append((name, len(P.ops["pe"])))
        if upto == name:
            raise _Stop()

    def setup():
      if True:
        P.dma("sp", identf[:], identf_d, writes=["identf"])
        P.dma("sp", jflip[:], jflip_d, writes=["jflip"])
        P.dma("sp", vrows[:], vrows_d, writes=["vrows"])
        P.dma("sp", gfin[:], bass.AP(gfin_d.tensor, 0, [[0, 128], [1, D]]), writes=["gfin"])
        P.add("dve", lambda e: e.memset(rb33[:], 1.0), writes=["rb33"])
        P.dma("sp", rb33[0:32, :], relb_d, reads=[], writes=["rb33"])
        P.add("dve", lambda e: e.memset(onesb[:], 1.0), writes=["onesb"])
        P.add("dve", lambda e: e.memset(chalf[:, :], -0.5), writes=["chalf"])
        P.add("dve", lambda e: e.memset(hst[:], 0.0), writes=["hst"])
        P.add("dve", lambda e: e.memset(XR[:], 0.0), writes=["XR"])
        P.add("pool", lambda e: e.memset(wa_bd[:], 0.0), writes=["wa_bd"])
        P.add("pool", lambda e: e.memset(wi_bd[:], 0.0), writes=["wi_bd"])
        dve_copy(identb[:], identf[:], ["identf"], ["identb"])
        CK("s1")
        for bb in range(2):
            for (dst, src, nm) in ((wa_bd, lwa_d, "wa_bd"), (wi_bd, lwi_d, "wi_bd")):
                P.dma("pool", dst[64 * bb:64 * bb + 64, :, 64 * bb:64 * bb + 64],
                      src.rearrange("(c b) e f -> b e c f", b=2)[bb], writes=[nm], lanes=[f"lw{bb}{nm}"])
        CK("s2")
        mm(ps[0][:, 0:NROWS], vrows[:, :], identf[0:NROWS, 0:NROWS], True, True, ["vrows", "identf"], [("ps", 0)])
        dve_copy(colv[:], ps[0][:, 0:NROWS], [("ps", 0)], ["colv"])
        dve_ts(dcol[:, 0:24], colv[:, R_BG:R_BG + 24], 0.5, None, ALU.mult, None, ["colv"], ["dcol"])
        dve_ts(dcol[:, 24:36], colv[:, R_BA:R_BA + 12], 0.5, None, ALU.mult, None, ["colv"], ["dcol"])
        act(dcol[:, 42:48], colv[:, R_LAM:R_LAM + 6], AF.Exp, ["colv"], ["dcol2"], scale=-1.0)
        act(dcol[:, 48:54], dcol[:, 42:48], AF.Ln, ["dcol2"], ["dcol3"], bias=1.0)
        dve_ts(dcol[:, 36:42], dcol[:, 48:54], -4.0, None, ALU.mult, None, ["dcol3"], ["dcol"])

        CK("s3")

    def setup_bias():
      if True:
        for g in range(3):
            so = nscr()
            P.dma("sp", scr[so][0:33, 0:384], oh_d[g], writes=[("scr", so)])
            mm(ps[1][0:4, 0:384], rb33[:, 4 * g:4 * g + 4], scr[so][0:33, 0:384], True, True, ["rb33", ("scr", so)], [("ps", 1)])
            si = nscr()
            dve_copy(scr[si][0:4, 0:384], ps[1][0:4, 0:384], [("ps", 1)], [("scr", si)])
            P.dma("sp", fx_d[4 * g:4 * g + 4, :], scr[si][0:4, 0:384], reads=[("scr", si)], writes=["fx"])
        for g in (1, 2):
            so = nscr()
            P.dma("sp", scr[so][0:33, 0:384], oh_d[3], writes=[("scr", so)])
            mm(ps[1][0:4, 0:384], rb33[:, 4 * g:4 * g + 4], scr[so][0:33, 0:384], True, True, ["rb33", ("scr", so)], [("ps", 1)])
            si = nscr()
            dve_copy(scr[si][0:4, 0:384], ps[1][0:4, 0:384], [("ps", 1)], [("scr", si)])
            P.dma("sp", fx_d[8 + 4 * g:12 + 4 * g, :], scr[si][0:4, 0:384], reads=[("scr", si)], writes=["fx"])
        P.add("dve", lambda e: e.memset(onesf[:], 1.0), writes=["onesf"])
        si = nscr()
        for g in (1, 2):
            for h in range(4):
                jc = (g - 1) * 4 + h
                P.dma("sp", scr[si][:, jc:jc + 1], bass.AP(fx_d.tensor, (4 * g + h) * 384 + 128, [[1, 128], [1, 1]]),
                      reads=["fx"] + ([("scr", si)] if jc else []), writes=[("scr", si) if jc == 0 else ("scr", si, jc)])
        for h in range(4):
            P.dma("sp", scr[si][:, 8 + 8 * h:16 + 8 * h], bass.AP(fx_d.tensor, h * 384 + 128, [[1, 128], [1, 8]]),
                  reads=["fx", ("scr", si)], writes=[("scr", si, 8 + h)])
        b = genbank()
        mm(ps[b][:, 0:40], jflip[:, :], scr[si][:, 0:40], True, True, ["jflip", ("scr", si)] + [("scr", si, q) for q in range(1, 12)], [("ps", b)])
        dve_copy(sbias[:, :], ps[b][:, 0:40], [("ps", b)], ["sbias"])
        si = nscr()
        for g in range(3):
            for h in range(4):
                row = h if g == 0 else 12 + 4 * (g - 1) + h
                jc = g * 4 + h
                P.dma("sp", scr[si][0:8, 8 * jc:8 * jc + 8], bass.AP(fx_d.tensor, row * 384 + 120, [[1, 8], [1, 8]]),
                      reads=["fx"] + ([("scr", si)] if jc else []), writes=[("scr", si) if jc == 0 else ("scr", si, jc)])
        b = genbank()
        mm(ps[b][0:8, 0:96], jflip[0:8, 120:128], scr[si][0:8, 0:96], True, True, ["jflip", ("scr", si)] + [("scr", si, q) for q in range(1, 12)], [("ps", b)])
        dve_copy(sbiasN[:, :], ps[b][0:8, 0:96], [("ps", b)], ["sbiasN"])
        CK("s4")
        for g in range(3):
            for typ in range(2):
                si = nscr()
                for h in range(4):
                    hank = bass.AP(fx_d.tensor, (4 * g + h) * 384 + 128 * typ, [[1, 128], [1, 128]])
                    P.dma("sp", scr[si][:, 128 * h:128 * h + 128], hank, reads=["fx"] + ([("scr", si)] if h else []), writes=[("scr", si) if h == 0 else ("scr", si, h)])
                if g == 0 and typ == 0:
                    CK("s5")
                b = genbank()
                mm(ps[b][:, :], jflip[:, :], scr[si][:, :], True, True, ["jflip", ("scr", si)] + [("scr", si, h) for h in range(4)], [("ps", b)])
                evac(biasT[:, g, typ, :, :], ps[b][:, :].rearrange("p (h q) -> p h q", h=4), [("ps", b)], [("biasT", g, typ)])

    class Ctx:
        pass


    def norm_stats(cx):
        nr, ntb = cx.nr, cx.ntb
        for tb in range(ntb):
            act(junk[0:nr, :], xres[0][0:nr, tb, :], AF.Square, xk(tb), JK + [("ss", tb)], accum_out=stat[0:nr, tb:tb + 1])
            dve_ts(stat[0:nr, 8 + tb:9 + tb], stat[0:nr, tb:tb + 1], 1.0 / D, 1e-6, ALU.mult, ALU.add, [("ss", tb)], [("ms", tb)])
            P.add("pool", lambda e, o=stat[0:nr, 16 + tb:17 + tb], i0=stat[0:nr, 8 + tb:9 + tb], i1=chalf[0:nr, 0:1]: e.tensor_tensor(out=o, in0=i0, in1=i1, op=ALU.pow),
                  [("ms", tb), "chalf"], [("rstd", tb)])

    def norm_to_hT(cx, gcol):
        nr, ntb, N = cx.nr, cx.ntb, cx.N
        norm_stats(cx)
        CK("m2a")
        for tb in range(ntb):
            if tb % 2 == 0:
                dve_ts(AT[0:nr, tb * D:(tb + 1) * D], xres[0][0:nr, tb, :], stat[0:nr, 16 + tb:17 + tb], None, ALU.mult, None,
                       xk(tb) + [("rstd", tb)], [("actT", 2 * tb), ("actT", 2 * tb + 1)])
            else:
                act(AT[0:nr, tb * D:(tb + 1) * D], xres[0][0:nr, tb, :], AF.Copy, xk(tb) + [("rstd", tb)],
                    [("actT", 2 * tb), ("actT", 2 * tb + 1)], scale=stat[0:nr, 16 + tb:17 + tb])
        CK("m2b")
        for kc in range(8):
            b = genbank()
            for tb in range(ntb):
                mm(ps[b][:, tb * 128:tb * 128 + nr], AT[0:nr, tb * D + kc * 128:tb * D + (kc + 1) * 128], identb[0:nr, 0:nr], True, True,
                   [("actT", 2 * tb + kc // 4), "identb"], [("ps", b)])
            evac(hT[:, kc, 0:N], ps[b][:, 0:N], [("ps", b), "colv"], [("hT", kc)], scale=colv[:, gcol + kc:gcol + kc + 1])

    def ffn(cx, wgu, wd):
        nr, ntb, N = cx.nr, cx.ntb, cx.N
        for ci in range(11):
            s = wload([(wgu[:, 256 * ci:256 * ci + 256], 0, 8, 256, 128), (wgu[:, DFF + 256 * ci:DFF + 256 * ci + 256], 256, 8, 256, 128)])
            for p2 in range(2):
                c = 2 * ci + p2
                bg, bu = genbank(), genbank()
                for kc in range(8):
                    mm(ps[bg][:, 0:N], wsl[s][:, kc, p2 * 128:(p2 + 1) * 128], hT[:, kc, 0:N], kc == 0, kc == 7, wk(s) + [("hT", kc)], [("ps", bg)])
                for kc in range(8):
                    mm(ps[bu][:, 0:N], wsl[s][:, kc, 256 + p2 * 128:256 + (p2 + 1) * 128], hT[:, kc, 0:N], kc == 0, kc == 7, wk(s) + [("hT", kc)], [("ps", bu)])
                si = nscr(6, 8)
                act(scr[si][:, 0:N], ps[bg][:, 0:N], AF.Silu, [("ps", bg)], [("scr", si)])
                dve_tt(actc(c)[:, 0:N], scr[si][:, 0:N], ps[bu][:, 0:N], ALU.mult, [("scr", si), ("ps", bu)], [("actT", c)])
        for half in range(2):
            for g3 in range(3):
                kcs = list(range(8 * g3, min(8 * g3 + 8, 22)))
                s = wload([(wd[128 * kcs[0]:128 * (kcs[-1] + 1), 512 * half:512 * half + 512], 0, len(kcs), 512, 128)])
                for tb in range(ntb):
                    pb_ = (4 if half == 0 else 0) + tb
                    for i, kc in enumerate(kcs):
                        mm(ps[pb_][0:nr, :], actc(kc)[:, tb * 128:tb * 128 + nr], wsl[s][:, i, 0:512], kc == 0, kc == 21,
                           wk(s) + [("actT", kc)], [("ps", pb_)])
            for tb in range(ntb):
                pb_ = (4 if half == 0 else 0) + tb
                xs_ = xres[0][0:nr, tb, half * 512:(half + 1) * 512]
                dve_stt(xs_, ps[pb_][0:nr, :], 0.5, xs_, ALU.mult, ALU.add, [("ps", pb_), ("x", tb, half)], [("x", tb, half)])

    def proj_fm(cx, col0, ncols, consume):
        N = cx.N
        nch = ncols // 128
        done = 0
        while done < nch:
            n = min(3 if ncols % 384 == 0 else 4, nch - done)
            s = wload([(win_d[:, col0 + 128 * done:col0 + 128 * (done + n)], 0, 8, 128 * n, 128)])
            for oc in range(n):
                b = genbank()
                for kc in range(8):
                    mm(ps[b][:, 0:N], wsl[s][:, kc, oc * 128:(oc + 1) * 128], hT[:, kc, 0:N], kc == 0, kc == 7, wk(s) + [("hT", kc)], [("ps", b)])
                consume(done + oc, b)
            yield s, done, n
            done += n


    TIX = (12, 13, 14, 15, 20, 21)

    def tix_buf(cx, c):
        if cx.mode != "prompt":
            return actc(TIX[c])[:, 0:cx.N], [("actT", TIX[c])]
        if c < 2:
            return QM[:, 6 + c, 0:cx.N], [("QM", 6 + c)]
        if c < 4:
            return actc(18 + c)[:, 0:cx.N], [("actT", 18 + c)]
        v2f = V2s[:, :, :].rearrange("p r c -> p (r c)")
        return v2f[:, 512 * (c - 4):512 * (c - 4) + cx.N], [("V2s", 2 * (c - 4)), ("V2s", 2 * (c - 4) + 1)]

    def lru_abuf(cx, c):
        if cx.mode == "prompt":
            return XR[:, c, 3:3 + cx.N], ("XR", c)
        return scr[c][:, 0:cx.N], ("scr", c)

    def lru_p1_stages(cx, c):
        N, nseq, L = cx.N, cx.nseq, cx.L
        ixc, iti, ixb = 6 + c % 2, 4 + c % 2, 3
        ab, akey = lru_abuf(cx, c)
        ti_buf, ti_key = (scr[iti][:, 0:N], ("scr", iti)) if cx.mode == "prompt" else (pbb[c % 2][:, 0:N], ("pb", c % 2))
        xc = scr[ixc][:, 0:N]
        xc3 = xc.rearrange("p (s l) -> p s l", s=nseq)
        xrv = XR[:, c, 0:nseq * (3 + L)].rearrange("p (s l) -> p s l", s=nseq)
        bk = 3

        def s0():
            act(xc3, xrv[:, :, 3:3 + L], AF.Identity, [("XR", c)], [("scr", ixc)],
                scale=colv[:, R_CW + 18 + c:R_CW + 19 + c], bias=colv[:, R_CB + c:R_CB + c + 1])

        def s1():
            for jt in range(3):
                dve_stt(xc3, xrv[:, :, jt:jt + L], colv[:, R_CW + 6 * jt + c:R_CW + 6 * jt + c + 1], xc3, ALU.mult, ALU.add,
                        [("XR", c), ("scr", ixc), "colv"], [("scr", ixc)])
            if cx.mode == "prompt":
                act(XR[:, c, 0:3], XR[:, c, L:L + 3], AF.Copy, [("XR", c)], [("XR", c)])
            dve_copy(pbb[ixb][:, 0:N], xc, [("scr", ixc)], [("pb", ixb)])

        def s2():
            mm(ps[bk][:, 0:N], wa_bd[:, c, :], pbb[ixb][:, 0:N], True, True, ["wa_bd", ("pb", ixb)], [("ps", bk)])

        def s3():
            act(ab, ps[bk][:, 0:N], AF.Tanh, [("ps", bk), "dcol"], [akey], scale=0.5, bias=dcol[:, C_HBA + c:C_HBA + c + 1])

        def s4():
            mm(ps[bk][:, 0:N], wi_bd[:, c, :], pbb[ixb][:, 0:N], True, True, ["wi_bd", ("pb", ixb)], [("ps", bk)])

        def s5():
            act(ab, ab, AF.Exp, [akey, "dcol"], [akey],
                scale=dcol[:, C_M4SP + c:C_M4SP + c + 1], bias=dcol[:, C_M4SP + c:C_M4SP + c + 1])
            act(ti_buf, ps[bk][:, 0:N], AF.Tanh, [("ps", bk), "dcol"], [ti_key], scale=0.5, bias=dcol[:, C_HBI + c:C_HBI + c + 1])

        def s6():
            tb_, tk_ = tix_buf(cx, c)
            dve_stt(tb_, ti_buf, 1.0, xc, ALU.add, ALU.mult, [ti_key, ("scr", ixc)], tk_)

        return [s0, s1, s2, s3, s4, s5, s6]

    def lru_p1(cx, c):
        for f in lru_p1_stages(cx, c):
            f()

    def lru_p1_slots(cx, stagger=4):
        slots = {}
        offs = (0, 1, 3, 4, 5, 6, 7)
        for c in range(6):
            for si_, f in enumerate(lru_p1_stages(cx, c)):
                slots.setdefault(stagger * c + offs[si_], []).append(f)
        out = []
        for t in range(max(slots) + 1):
            fs = slots.get(t, [])
            out.append(lambda fs=fs: [f() for f in fs])
        return out

    def lru_p2(cx):
        N, nseq, L = cx.N, cx.nseq, cx.L
        for c in range(6):
            ab, akey = lru_abuf(cx, c)
            iy = 6 + c % 2
            yb = scr[iy][:, 0:N]
            dve_tt(yb, ab, ab, ALU.mult, [akey], [("scr", iy)])
            act(yb, yb, AF.Sqrt, [("scr", iy)], [("scr", iy)], scale=-1.0, bias=1.0)
            tb_, tk_ = tix_buf(cx, c)
            dve_stt(yb, yb, 0.5, tb_, ALU.mult, ALU.mult, [("scr", iy)] + tk_, [("scr", iy)])
            for sq in range(nseq):
                sl = slice(sq * L, (sq + 1) * L)
                P.add("dve", lambda e, o=scr[iy][:, sl], a=ab[:, sl], u=scr[iy][:, sl], i0=hst[:, c, sq:sq + 1]:
                      e.tensor_tensor_scan(out=o, data0=a, data1=u, initial=i0, op0=ALU.mult, op1=ALU.add),
                      [akey, ("scr", iy), ("hst", c)], [("scr", iy)])
            hv = scr[iy][:, 0:N].rearrange("p (s l) -> p s l", s=nseq)[:, :, L - 1]
            dve_copy(hst[:, c, 0:nseq], hv, [("scr", iy)], [("hst", c)])
            dve_tt(actc(6 + c)[:, 0:N], actc(c)[:, 0:N], scr[iy][:, 0:N], ALU.mult, [("actT", c), ("scr", iy)], [("actT", 6 + c)])

    def lru_p2a(cx):
        N = cx.N
        for c in range(6):
            ab, akey = lru_abuf(cx, c)
            act(scr[c][:, 0:N], ab, AF.Square, [akey], [("scr", c)])
            act(scr[c][:, 0:N], scr[c][:, 0:N], AF.Sqrt, [("scr", c)], [("scr", c)], scale=-1.0, bias=1.0)

    def lru_p2b(cx, c):
        N, nseq, L = cx.N, cx.nseq, cx.L
        ab, akey = lru_abuf(cx, c)
        yb = scr[c][:, 0:N]
        tb_, tk_ = tix_buf(cx, c)
        dve_stt(yb, yb, 0.5, tb_, ALU.mult, ALU.mult, [("scr", c)] + tk_, [("scr", c)])
        for sq in range(nseq):
            sl = slice(sq * L, (sq + 1) * L)
            P.add("dve", lambda e, o=scr[c][:, sl], a=ab[:, sl], u=scr[c][:, sl], i0=hst[:, c, sq:sq + 1]:
                  e.tensor_tensor_scan(out=o, data0=a, data1=u, initial=i0, op0=ALU.mult, op1=ALU.add),
                  [akey, ("scr", c), ("hst", c)], [("scr", c)])
        hv = scr[c][:, 0:N].rearrange("p (s l) -> p s l", s=nseq)[:, :, L - 1]
        dve_copy(hst[:, c, 0:nseq], hv, [("scr", c)], [("hst", c)])
        dve_tt(actc(6 + c)[:, 0:N], actc(c)[:, 0:N], scr[c][:, 0:N], ALU.mult, [("actT", c), ("scr", c)], [("actT", 6 + c)])

    def lru_phase(cx):
        for c in range(6):
            lru_p1(cx, c)
        lru_p2(cx)

    def mem_attn(cx, mk_of, mv_of, nqs, mk_keys=lambda bi, h: [("mkT", bi)]):
        for h in range(4):
            Ob, Lb = 4 + (h % 2), 6 + (h % 2)
            for (c0, ncol, bi) in nqs:
                for blk in range(2):
                    b = genbank()
                    mm(ps[b][:, c0:c0 + ncol], mk_of(bi)[:, h, blk * 128:(blk + 1) * 128], qmT[:, h, c0:c0 + ncol], True, True,
                       mk_keys(bi, h) + [("qmT", h)], [("ps", b)])
                    pi = nscr(0, 4)
                    act(pbb[pi][:, c0:c0 + ncol], ps[b][:, c0:c0 + ncol], AF.Exp, [("ps", b)], [("pb", pi)], scale=float(128 ** -0.5))
                    mm(ps[Ob][:, c0:c0 + ncol], mv_of(bi)[:, blk, h, :], pbb[pi][:, c0:c0 + ncol], blk == 0, blk == 1, [("mv", bi), ("pb", pi)], [("ps", Ob)])
                    mm(ps[Lb][:, c0:c0 + ncol], onesb[:, :], pbb[pi][:, c0:c0 + ncol], blk == 0, blk == 1, ["onesb", ("pb", pi)], [("ps", Lb)])
            N = cx.N
            si = nscr(4, 6)
            P.add("dve", lambda e, o=scr[si][:, 0:N], i=ps[Lb][:, 0:N]: e.reciprocal(out=o, in_=i), [("ps", Lb)], [("scr", si)])
            dve_tt(actc(12 + h)[:, 0:N], ps[Ob][:, 0:N], scr[si][:, 0:N], ALU.mult, [("ps", Ob), ("scr", si)], [("actT", 12 + h)])

    def merge_and_out(cx, hooks=()):
        nr, ntb, N = cx.nr, cx.ntb, cx.N
        hooks = list(hooks)
        for c4 in range(2):
            branches = (
                (GA0, C_HBG + 0, [(wao_d[:, 512 * c4:512 * c4 + 512], 0, 4, 512, 64)], [(h, 64, ("actT", 16 + h), lambda h=h: actc(16 + h)[0:64, 0:N]) for h in range(4)]),
                (GB0, C_HBG + 8, [(wro_d[:, 512 * c4:512 * c4 + 512], 0, 6, 512, 128)], [(k, 128, ("actT", 6 + k), lambda k=k: actc(6 + k)[:, 0:N]) for k in range(6)]),
                (GC0, C_HBG + 16, [(wmo_d[:, 512 * c4:512 * c4 + 512], 0, 4, 512, 128)], [(k, 128, ("actT", 12 + k), lambda k=k: actc(12 + k)[:, 0:N]) for k in range(4)]),
            )
            for bi, (gcol0, hb0, oparts, kparts) in enumerate(branches):
                s = wload([(win_d[:, gcol0 + 512 * c4:gcol0 + 512 * c4 + 512], 0, 8, 512, 128)])
                for oc in range(4):
                    b = genbank()
                    for kc in range(8):
                        mm(ps[b][:, 0:N], wsl[s][:, kc, oc * 128:(oc + 1) * 128], hT[:, kc, 0:N], kc == 0, kc == 7, wk(s) + [("hT", kc)], [("ps", b)])
                    act(pbb[oc][:, 0:N], ps[b][:, 0:N], AF.Tanh, [("ps", b), "dcol"], [("pb", oc)], scale=0.5,
                        bias=dcol[:, hb0 + 4 * c4 + oc:hb0 + 4 * c4 + oc + 1])
                    if hooks:
                        hooks.pop(0)()
                s2 = wload(oparts)
                for oc in range(4):
                    b = genbank()
                    nkp = len(kparts)
                    for (ki, kp, key, apf) in kparts:
                        mm(ps[b][:, 0:N], wsl[s2][0:kp, ki, oc * 128:(oc + 1) * 128], apf(), ki == 0, ki == nkp - 1, wk(s2) + [key], [("ps", b)])
                    if bi == 0:
                        dve_stt(scr[oc][:, 0:N], pbb[oc][:, 0:N], 1.0, ps[b][:, 0:N], ALU.add, ALU.mult, [("pb", oc), ("ps", b)], [("scr", oc)])
                        if hooks:
                            hooks.pop(0)()
                    else:
                        ti = 4 + (oc % 2)
                        dve_stt(scr[ti][:, 0:N], pbb[oc][:, 0:N], 1.0, ps[b][:, 0:N], ALU.add, ALU.mult, [("pb", oc), ("ps", b)], [("scr", ti)])
                        if bi == 1:
                            dve_tt(scr[oc][:, 0:N], scr[oc][:, 0:N], scr[ti][:, 0:N], ALU.add, [("scr", oc), ("scr", ti)], [("scr", oc)])
                        else:
                            dve_tt(QM[:, 4 * c4 + oc, 0:N], scr[oc][:, 0:N], scr[ti][:, 0:N], ALU.add, [("scr", oc), ("scr", ti)], [("QM", 4 * c4 + oc)])
        for half in range(2):
            s = wload([(wout_d[:, 512 * half:512 * half + 512], 0, 8, 512, 128)])
            for tb in range(ntb):
                pb_ = (4 if half == 0 else 0) + tb
                for kc in range(8):
                    mm(ps[pb_][0:nr, :], QM[:, kc, tb * 128:tb * 128 + nr], wsl[s][:, kc, 0:512], kc == 0, kc == 7, wk(s) + [("QM", kc)], [("ps", pb_)])
            for tb in range(ntb):
                pb_ = (4 if half == 0 else 0) + tb
                xs_ = xres[0][0:nr, tb, half * 512:(half + 1) * 512]
                dve_stt(xs_, ps[pb_][0:nr, :], 0.5, xs_, ALU.mult, ALU.add, [("ps", pb_), ("x", tb, half)], [("x", tb, half)])

    def final_norm_store(cx, ydst, after_tb=None):
        nr, ntb = cx.nr, cx.ntb
        norm_stats(cx)
        for tb in range(ntb):
            ys_ = ystage[tb % 2]
            dve_stt(ys_[0:nr, :], xres[0][0:nr, tb, :], stat[0:nr, 16 + tb:17 + tb], gfin[0:nr, :], ALU.mult, ALU.mult,
                    xk(tb) + [("rstd", tb), "gfin"], [("yst", tb % 2)])
            if cx.mode == "prompt":
                P.dma("sp", ydst[tb * 128:(tb + 1) * 128, :], ys_[:, :], reads=[("yst", tb % 2)])
            else:
                P.dma("sp", ydst, ys_[0:nr, :], reads=[("yst", tb % 2)])
            if after_tb is not None:
                after_tb(tb)

    def cache_shift_copies():
        for g in range(3):
            Lc = GROUPS[g][0]
            for (src, dst) in ((ck_d[g], sk_d[g]), (cv_d[g], sv_d[g])):
                for b in range(NSB):
                    npr = (Lc - ST) * 16
                    P.dma("sp", bass.AP(dst.tensor, b * Lc * 256, [[npr, 16], [1, npr]]),
                          bass.AP(src.tensor, b * Lc * 256 + ST * 256, [[npr, 16], [1, npr]]))

    def sample_phase():
        cx = Ctx(); cx.nr, cx.ntb, cx.N, cx.mode, cx.nseq, cx.L = NS, 1, NS, "sample", NSB, ST
        N = NS
        P.barrier()
        Kg = [k2[:, i, :].bitcast(F32).rearrange("p (t c) -> p t c", t=8) for i in range(2)]
        Vg = [V2[:, i, :, :].rearrange("p r c -> p (r c)").bitcast(F32).rearrange("p (t c) -> p t c", t=8) for i in range(2)]
        qbuf = k01[:, :, :, :].rearrange("p a b n -> p (a b n)").bitcast(F32).rearrange("p (t c) -> p t c", t=8)
        q_tm = V0[:, :, :, :].rearrange("p a b n -> p (a b n)").bitcast(F32)
        k_tm = V1[:, :, :, :].rearrange("p a b n -> p (a b n)").bitcast(F32)
        bflat = biasT[:, :, :, :, :].rearrange("p g t h q -> p (g t h q)")
        v_tm = bflat[:, 0:768]
        K0s, V0s = bflat[:, 1024:1280], bflat[:, 1280:1536]
        KN, VN = bflat[0:8, 1536:2304], bflat[0:8, 2304:3072]
        mkT_s = QM[:, :, :].rearrange("p c n -> p (c n)").rearrange("p (b h k) -> p b h k", b=NSB, h=4)
        mvb_s = k2[:, 0, :].rearrange("p (b k h e) -> p b k h e", b=NSB, k=2, h=4)

        si = nscr()
        P.dma("sp", scr[si][0:72, 0:128], sconv_d, writes=[("scr", si)])
        b = genbank()
        mm(ps[b][:, 0:72], scr[si][0:72, 0:128], identf[0:72, 0:72], True, True, [("scr", si), "identf"], [("ps", b)])
        xh = XR[:, :, 0:NSB * (3 + ST)].rearrange("p c (s l) -> p c s l", s=NSB)[:, :, :, 0:3]
        P.add("dve", lambda e, o=xh, i_=ps[b][:, 0:72].rearrange("p (s l c) -> p c s l", s=NSB, l=3): e.tensor_copy(out=o, in_=i_),
              [("ps", b)], [("XR", c) for c in range(6)])
        si = nscr()
        P.dma("sp", scr[si][0:24, 0:128], slru_d, writes=[("scr", si)])
        b2 = genbank()
        mm(ps[b2][:, 0:24], scr[si][0:24, 0:128], identf[0:24, 0:24], True, True, [("scr", si), "identf"], [("ps", b2)])
        P.add("dve", lambda e, i_=ps[b2][:, 0:24].rearrange("p (s c) -> p c s", s=NSB): e.tensor_copy(out=hst[:, :, :], in_=i_),
              [("ps", b2)], [("hst", c) for c in range(6)])

        P.add("dve", lambda e: e.memset(xres[0][:, 0, :], 0.0), writes=xk(0))
        P.dma("sp", xres[0][0:NS, 0, :], xs_d, writes=xk(0))
        norm_to_hT(cx, R_G1)
        ffn(cx, w1gu_d, w1d_d)
        norm_to_hT(cx, R_GM)
        for _ in proj_fm(cx, YR0, 768, lambda c, b: act(actc(c)[:, 0:N], ps[b][:, 0:N], AF.Gelu_apprx_tanh, [("ps", b)], [("actT", c)])):
            pass
        for (seg, dst, key, scl) in ((Q0, q_tm, "q_tm", 0.125), (K0, k_tm, "k_tm", None), (V0c, v_tm, "v_tm", None)):
            for ch in range(2):
                s = wload([(win_d[:, seg + 384 * ch:seg + 384 * ch + 384], 0, 8, 384, 128)])
                b = genbank()
                for kc in range(8):
                    mm(ps[b][0:NS, 0:384], hT[:, kc, 0:NS], wsl[s][:, kc, 0:384], kc == 0, kc == 7, wk(s) + [("hT", kc)], [("ps", b)])
                evac(dst[0:NS, 384 * ch:384 * ch + 384], ps[b][0:NS, 0:384], [("ps", b)], [(key, ch)], scale=scl)
        P.dma("sp", qs_d[:, :], q_tm[0:NS, 0:768], reads=[("q_tm", 0), ("q_tm", 1)], writes=["qs"])
        for g in range(3):
            Lc = GROUPS[g][0]
            for (src, dst, key) in ((k_tm, sk_d[g], "k_tm"), (v_tm, sv_d[g], "v_tm")):
                for b in range(NSB):
                    P.dma("sp", dst[b, Lc - ST:Lc, :], src[ST * b:ST * b + ST, 256 * g:256 * g + 256], reads=[(key, 0), (key, 1)])
        def xr_consume(c, b):
            o = XR[:, c, 0:NSB * (3 + ST)].rearrange("p (s l) -> p s l", s=NSB)[:, :, 3:3 + ST]
            evac(o, ps[b][:, 0:N].rearrange("p (s l) -> p s l", s=NSB), [("ps", b)], [("XR", c)])
        for (s, done, n) in proj_fm(cx, XR0, 768, xr_consume):
            b = genbank()
            for kc in range(8):
                mm(ps[b][0:NS, 0:384], hT[:, kc, 0:NS], wsl[s][:, kc, 0:384], kc == 0, kc == 7, wk(s) + [("hT", kc)], [("ps", b)])
            si = nscr()
            dve_copy(scr[si][0:NS, 0:384], ps[b][0:NS, 0:384], [("ps", b)], [("scr", si)])
            for bb in range(NSB):
                P.dma("sp", sconvo_d[bb, :, 128 * done:128 * done + 384], scr[si][ST * bb + 5:ST * bb + 8, 0:384], reads=[("scr", si)])
        for _ in proj_fm(cx, QM0, 512, lambda c, b: evac(qmT[:, c, 0:N], ps[b][:, 0:N], [("ps", b)], [("qmT", c)])):
            pass
        CK("s_front")

        OSb, LSb = 4, 5
        st8 = {"o": True, "l": True}
        S_i, P_i, SN_i, PN_i = 0, 1, 2, 3
        for bq in range(NSB):
            P.dma("sp", KN, k_tm[ST * bq:ST * bq + ST, 0:768], reads=[("k_tm", 0), ("k_tm", 1)], writes=["KN"])
            P.dma("sp", VN, v_tm[ST * bq:ST * bq + ST, 0:768], reads=[("v_tm", 0), ("v_tm", 1)], writes=["VN"])
            for g in range(3):
                Lc, dil = GROUPS[g]
                i = (3 * bq + g) % 2
                Kt, Vt = Kg[i], Vg[i]
                P.dma("sp", qbuf, bass.AP(qs_d.tensor, ST * bq * 768 + 256 * g, [[0, 128], [768, 8], [1, 256]]), reads=["qs"], writes=["qbuf"])
                if g == 0:
                    P.dma("sp", K0s, ck_d[0][bq], writes=["K0s"])
                    P.dma("sp", V0s, cv_d[0][bq], writes=["V0s"])
                    dve_tt(Kt, bc_mid(K0s, 8), qbuf, ALU.mult, ["K0s", "qbuf"], [("Kt", i)])
                else:
                    for (T_, src, tm, key) in ((Kt, ck_d[g], k_tm, "Kt"), (Vt, cv_d[g], v_tm, "Vt")):
                        if g == 1:
                            P.dma("sp", T_[0:112, :, :], bass.AP(src.tensor, bq * Lc * 256, [[4 * 256, 112], [256, 8], [1, 256]]), writes=[(key, i)])
                            P.dma("sp", T_[112:127, :, :], bass.AP(src.tensor, bq * Lc * 256 + 112 * 4 * 256, [[4 * 256, 15], [256, 8], [1, 256]]),
                                  reads=[(key, i)], writes=[(key, i, "c")])
                            P.dma("sp", T_[127:128, 0:4, :], bass.AP(src.tensor, bq * Lc * 256 + 508 * 256, [[256, 1], [256, 4], [1, 256]]),
                                  reads=[(key, i)], writes=[(key, i, "a")])
                            for t4 in range(4):
                                P.dma("sp", T_[127:128, 4 + t4, :], tm[ST * bq + t4:ST * bq + t4 + 1, 256:512],
                                      reads=[(key, i), ("k_tm", 0), ("k_tm", 1), ("v_tm", 0), ("v_tm", 1)], writes=[(key, i, "b", t4)])
                        else:
                            P.dma("sp", T_, src[bq].rearrange("(m r) c -> m r c", r=16)[:, 0:8, :], writes=[(key, i)])
                    dve_tt(Kt, Kt, qbuf, ALU.mult, [("Kt", i), ("Kt", i, "a"), ("Kt", i, "c")] + [("Kt", i, "b", t4) for t4 in range(4)] + ["qbuf"], [("Kt", i)])
                P.add("dve", lambda e, o=scr[S_i][:, 0:32], a=Kt.rearrange("p t (h e) -> p (t h) e", h=4): e.tensor_reduce(out=o, in_=a, axis=mybir.AxisListType.X, op=ALU.add),
                      [("Kt", i)], [("scr", S_i)])
                S3 = scr[S_i][:, 0:32].rearrange("p (t h) -> p t h", h=4)
                if g == 0:
                    bia = sbias[:, 8:40].rearrange("p (h t) -> p t h", t=8)
                else:
                    bia = bc_mid(sbias[:, 4 * (g - 1):4 * (g - 1) + 4], 8)
                dve_tt(S3, S3, bia, ALU.add, [("scr", S_i), "sbias"], [("scr", S_i)])
                act(scr[P_i][:, 0:32], scr[S_i][:, 0:32], AF.Exp, [("scr", S_i)], [("scr", P_i)])
                if g == 0:
                    for h in range(4):
                        mm(ps[OSb][0:64, 32 * bq + h:32 * bq + 32:4], V0s[:, 64 * h:64 * h + 64], scr[P_i][:, h:32:4], st8["o"], False,
                           ["V0s", ("scr", P_i)], [("ps", OSb)])
                        st8["o"] = False
                else:
                    for t in range(8):
                        for h in range(4):
                            cidx = 32 * bq + 4 * t + h
                            mm(ps[OSb][0:64, cidx:cidx + 1], Vt[:, t, 64 * h:64 * h + 64], scr[P_i][:, 4 * t + h:4 * t + h + 1], st8["o"], False,
                               [("Vt", i), ("Vt", i, "a"), ("Vt", i, "c")] + [("Vt", i, "b", t4) for t4 in range(4)] + [("scr", P_i)], [("ps", OSb)])
                            st8["o"] = False
                mm(ps[LSb][0:64, 32 * bq:32 * bq + 32], onesf[:, 0:64], scr[P_i][:, 0:32], st8["l"], False, ["onesf", ("scr", P_i)], [("ps", LSb)])
                st8["l"] = False
                q8 = bass.AP(qbuf.tensor, qbuf.offset, [[qbuf.ap[0][0], 8]] + [list(a) for a in qbuf.ap[1:]])
                dve_tt(q8, bc_mid(KN[:, 256 * g:256 * g + 256], 8), q8, ALU.mult, ["KN", "qbuf", ("Kt", i)], ["qbuf"])
                P.add("dve", lambda e, o=scr[SN_i][0:8, 0:32], a=q8.rearrange("p t (h e) -> p (t h) e", h=4): e.tensor_reduce(out=o, in_=a, axis=mybir.AxisListType.X, op=ALU.add),
                      ["qbuf"], [("scr", SN_i)])
                SN3 = scr[SN_i][0:8, 0:32].rearrange("p (t h) -> p t h", h=4)
                dve_tt(SN3, SN3, sbiasN[:, 32 * g:32 * g + 32].rearrange("p (h t) -> p t h", t=8), ALU.add, [("scr", SN_i), "sbiasN"], [("scr", SN_i)])
                act(scr[PN_i][0:8, 0:32], scr[SN_i][0:8, 0:32], AF.Exp, [("scr", SN_i)], [("scr", PN_i)])
                last = (bq == NSB - 1 and g == 2)
                for h in range(4):
                    mm(ps[OSb][0:64, 32 * bq + h:32 * bq + 32:4], VN[:, 256 * g + 64 * h:256 * g + 64 * h + 64], scr[PN_i][0:8, h:32:4], False, last and h == 3,
                       ["VN", ("scr", PN_i)], [("ps", OSb)])
                mm(ps[LSb][0:64, 32 * bq:32 * bq + 32], onesf[0:8, 0:64], scr[PN_i][0:8, 0:32], False, last, ["onesf", ("scr", PN_i)], [("ps", LSb)])
        si = nscr(4, 6)
        P.add("dve", lambda e, o=scr[si][0:64, 0:128], a=ps[LSb][0:64, 0:128]: e.reciprocal(out=o, in_=a), [("ps", LSb)], [("scr", si)])
        for h in range(4):
            dve_tt(actc(16 + h)[0:64, 0:N], ps[OSb][0:64, h:128:4], scr[si][0:64, h:128:4], ALU.mult, [("ps", OSb), ("scr", si)], [("actT", 16 + h)])
        CK("s_attn")

        lru_phase(cx)
        b = genbank()
        mm(ps[b][0:24, 0:128], hst[:, :, :].rearrange("p c s -> p (c s)"), identf[:, :], True, True, [("hst", c) for c in range(6)] + ["identf"], [("ps", b)])
        si = nscr()
        dve_copy(scr[si][0:24, 0:128], ps[b][0:24, 0:128], [("ps", b)], [("scr", si)])
        for c in range(6):
            P.dma("sp", slruo_d[c:NSB * 6:6, :], scr[si][NSB * c:NSB * c + NSB, 0:128], reads=[("scr", si)])
        CK("s_lru")

        for bq in range(NSB):
            kt = Vg[bq % 2].rearrange("p t c -> p (t c)")
            ktv = kt[:, 0:1024].rearrange("p (k c) -> p k c", k=2)
            P.dma("sp", ktv, cmk_d[bq].rearrange("(k p) c -> p k c", p=128), writes=[("Vt", bq % 2)])
            P.dma("pool", mvb_s[:, bq, :, :, :].rearrange("p k h e -> p k (h e)"), cmv_d[bq].rearrange("(k p) c -> p k c", p=128), writes=[("mv", 1 + bq), ("Kt", 0)], lanes=[f"smv{bq % 2}"])
            for h in range(4):
                b = genbank()
                for blk in range(2):
                    mm(ps[b][:, blk * 128:(blk + 1) * 128], ktv[:, blk, h * 128:(h + 1) * 128], identf[:, :], True, True, [("Vt", bq % 2), "identf"], [("ps", b)])
                evac(mkT_s[:, bq, h, :], ps[b][:, 0:256], [("ps", b)], [("QM", 2 * bq + h // 2)])
        mem_attn(cx, lambda bi: mkT_s[:, bi - 1, :, :], lambda bi: mvb_s[:, bi - 1, :, :, :], [(ST * bq, ST, 1 + bq) for bq in range(NSB)],
                 mk_keys=lambda bi, h: [("QM", 2 * (bi - 1) + h // 2)])
        CK("s_mem")
        merge_and_out(cx)
        CK("s_merge")
        norm_to_hT(cx, R_G2)
        ffn(cx, w2gu_d, w2d_d)
        CK("s_ffn2")
        final_norm_store(cx, ys_d[:, :])

    def prompt_memory():
        cx = Ctx(); cx.nr, cx.ntb, cx.N, cx.mode = 128, 2, 256, "prompt"
        P.dma("sp", xres[0][:, 0:2, :], mem_d.rearrange("(tb p) d -> p tb d", p=128), writes=xk(0) + xk(1))
        CK("m1")
        norm_to_hT(cx, R_GMEM)
        CK("m3")
        for half in range(2):
            s = wload([(wmkv_d[:, 512 * half:512 * half + 512], 0, 8, 512, 128)])
            for blk in range(2):
                b = genbank()
                for kc in range(8):
                    mm(ps[b][:, :], hT[:, kc, blk * 128:(blk + 1) * 128], wsl[s][:, kc, 0:512], kc == 0, kc == 7, wk(s) + [("hT", kc)], [("ps", b)])
                CK("m4")
                si = nscr()
                dve_copy(scr[si][:, :], ps[b][:, :], [("ps", b)], [("scr", si)])
                CK("m5")
                dst = (pmk_d if half == 0 else pmv_d)[blk * 128:(blk + 1) * 128, :]
                P.dma("sp", dst, scr[si][:, :], reads=[("scr", si)])
                if half == 1:
                    act(mvb[:, blk, :, :], ps[b][:, :].rearrange("p (h e) -> p h e", h=4), AF.Copy, [("ps", b)], [("mv", 0)])
            CK("m6")
            if half == 0:
                for h in range(4):
                    b = genbank()
                    for kc in range(8):
                        mm(ps[b][:, 0:256], wsl[s][:, kc, h * 128:(h + 1) * 128], hT[:, kc, 0:256], kc == 0, kc == 7, wk(s) + [("hT", kc)], [("ps", b)])
                    evac(mkT[:, h, :], ps[b][:, 0:256], [("ps", b)], [("mkT", 0)])
                CK("m7")

    def attention_prompt(j, hooks=()):
        s_, b_, slot = j % 4, j // 4, j % 2
        pslot = 1 - slot
        jobs = []

        def add_job(h, g, typ, blocks, npart, cmin, cmax, bias_ap, l_out, l_rhs_view):
            jobs.append(dict(h=h, g=g, typ=typ, blocks=blocks, npart=npart, cmin=cmin, cmax=cmax, bias_ap=bias_ap, l_out=l_out,
                             l_rhs_view=l_rhs_view, first=False, last=False))

        for h in range(4):
            hp, pr0 = h // 2, (h % 2) * 64
            prs = slice(pr0, pr0 + 64)
            Ob, Lb = 4 + (h % 2), 6 + (h % 2)
            n0 = len(jobs)
            c = hp
            blocks = []
            for qb in range(4):
                cs = slice(qb * 128, (qb + 1) * 128)
                blocks.append((k01[prs, c, slot, cs], QM[prs, c, cs], qb * 128, 128, V0[:, slot, qb, 64 * h:64 * h + 64], ps[Ob][0:64, cs],
                               [("k01", c, slot)], [("V0", slot)]))
            add_job(h, 0, 0, blocks, 128, 0, 512, bc_mid(biasT[:, 0, 0, h, :], 4), ps[Lb][0:64, 0:512], lambda a: a)
            blocks = []
            for qb in range(4):
                if qb == 0 and j == 0:
                    continue
                cs = slice(qb * 128, (qb + 1) * 128)
                if qb == 0:
                    kap, vap, kk, vk = k01[prs, c, pslot, 384:512], V0[:, pslot, 3, 64 * h:64 * h + 64], [("k01", c, pslot)], [("V0", pslot)]
                else:
                    ks = slice((qb - 1) * 128, qb * 128)
                    kap, vap, kk, vk = k01[prs, c, slot, ks], V0[:, slot, qb - 1, 64 * h:64 * h + 64], [("k01", c, slot)], [("V0", slot)]
                blocks.append((kap, QM[prs, c, cs], qb * 128, 128, vap, ps[Ob][0:64, cs], kk, vk))
            cmin = 128 if j == 0 else 0
            add_job(h, 0, 1, blocks, 128, cmin, 512, bc_mid(biasT[:, 0, 1, h, :], (512 - cmin) // 128), ps[Lb][0:64, cmin:512], lambda a: a)
            c = 2 + hp
            o4 = ps[Ob][0:64, :].rearrange("p (i r) -> p r i", r=4)
            l4 = ps[Lb][0:64, :].rearrange("p (i r) -> p r i", r=4)
            for typ in range(2):
                if typ == 1 and j == 0:
                    continue
                ksl = slot if typ == 0 else pslot
                blocks = []
                for r in range(4):
                    kap = k01[prs, c, ksl, :].rearrange("p (i r) -> p r i", r=4)[:, r, :]
                    qap = QM[prs, c, :].rearrange("p (i r) -> p r i", r=4)[:, r, :]
                    blocks.append((kap, qap, r * 128, 128, V1[:, ksl, r, 64 * h:64 * h + 64], o4[:, r, :], [("k01", c, ksl)], [("V1", ksl)]))
                add_job(h, 1, typ, blocks, 128, 0, 512, bc_mid(biasT[:, 1, typ, h, :], 4), l4, lambda a: a.rearrange("p (r i) -> p r i", r=4))
            c = 4 + hp
            o16 = ps[Ob][0:64, :].rearrange("p (a r) -> p r a", r=16)
            l16 = ps[Lb][0:64, :].rearrange("p (a r) -> p r a", r=16)
            npc = 32 * (s_ + 1)
            blocks = []
            for r in range(16):
                base = 2048 * b_ + r
                kap = k2[prs, hp, base:base + 16 * (npc - 1) + 1:16]
                qap = QM[prs, c, :].rearrange("p (a r) -> p r a", r=16)[:, r, :]
                blocks.append((kap, qap, r * 32, 32, V2[0:npc, b_, r, 64 * h:64 * h + 64], o16[:, r, :], [("k2", hp, t) for t in range(4 * b_, j + 1)], [("V2", b_)]))
            add_job(h, 2, 0, blocks, npc, 0, 512, bc_mid(biasT[0:npc, 2, 0, h, 32 * s_:32 * s_ + 32], 16), l16,
                    lambda a: a.rearrange("p (r a) -> p r a", r=16))
            if b_ >= 1:
                blocks = []
                for r in range(16):
                    base = 2048 * (b_ - 1) + r
                    kap = k2[prs, hp, base:base + 16 * 127 + 1:16]
                    qap = QM[prs, c, :].rearrange("p (a r) -> p r a", r=16)[:, r, :]
                    blocks.append((kap, qap, r * 32, 32, V2[:, b_ - 1, r, 64 * h:64 * h + 64], o16[:, r, :],
                                   [("k2", hp, t) for t in range(4 * (b_ - 1), 4 * b_)], [("V2", b_ - 1)]))
                add_job(h, 2, 1, blocks, 128, 0, 512, bc_mid(biasT[:, 2, 1, h, 32 * s_:32 * s_ + 32], 16), l16,
                        lambda a: a.rearrange("p (r a) -> p r a", r=16))
            jobs[n0]["first"] = True
            jobs[-1]["last"] = True

        rot = [0]

        def emit_scores(jb):
            h, g, typ, npart, cmin, cmax = jb["h"], jb["g"], jb["typ"], jb["npart"], jb["cmin"], jb["cmax"]
            hp = h // 2
            b = rot[0] % 3
            rot[0] += 1
            for (k_ap, q_ap, c0, nq, v_ap, o_ap, kkeys, vkeys) in jb["blocks"]:
                mm(ps[b][0:npart, c0:c0 + nq], k_ap, q_ap, True, True, kkeys + [("QM", 2 * g + hp)], [("ps", b)])
            si = rot[0] % 3
            P.add("dve", lambda e, o=scr[si][0:npart, cmin:cmax], i0=ps[b][0:npart, cmin:cmax], i1=jb["bias_ap"]:
                  e.tensor_tensor(out=o.rearrange("p (a q) -> p a q", a=i1.shape[1]), in0=i0.rearrange("p (a q) -> p a q", a=i1.shape[1]), in1=i1, op=ALU.add),
                  [("ps", b), ("biasT", g, typ)], [("scr", si)])
            pi = rot[0] % 3
            act(pbb[pi][0:npart, cmin:cmax], scr[si][0:npart, cmin:cmax], AF.Exp, [("scr", si)], [("pb", pi)])
            jb["pi"] = pi

        def emit_pv(jb):
            h, npart, cmin, cmax, pi = jb["h"], jb["npart"], jb["cmin"], jb["cmax"], jb["pi"]
            Ob, Lb = 4 + (h % 2), 6 + (h % 2)
            nblk = len(jb["blocks"])
            for bi2, (k_ap, q_ap, c0, nq, v_ap, o_ap, kkeys, vkeys) in enumerate(jb["blocks"]):
                mm(o_ap, v_ap, pbb[pi][0:npart, c0:c0 + nq], jb["first"] and bi2 == 0, jb["last"] and bi2 == nblk - 1, vkeys + [("pb", pi)], [("ps", Ob)])
            mm(jb["l_out"], onesb[0:npart, 0:64], jb["l_rhs_view"](pbb[pi][0:npart, cmin:cmax]), jb["first"], jb["last"], ["onesb", ("pb", pi)], [("ps", Lb)])
            if jb["last"]:
                si = 3
                P.add("dve", lambda e, o=scr[si][0:64, :], i=ps[Lb][0:64, :]: e.reciprocal(out=o, in_=i), [("ps", Lb)], [("scr", si)])
                dve_tt(actc(16 + h)[0:64, :], ps[Ob][0:64, :], scr[si][0:64, :], ALU.mult, [("ps", Ob), ("scr", si)], [("actT", 16 + h)])

        LOOK = 2
        hooks = list(hooks)
        for k in range(len(jobs) + LOOK):
            if k < len(jobs):
                emit_scores(jobs[k])
            if k - LOOK >= 0:
                emit_pv(jobs[k - LOOK])
            if hooks:
                hooks.pop(0)()
        while hooks:
            hooks.pop(0)()

    PRELOAD = [False]

    def prompt_tile(j):
        cx = Ctx(); cx.nr, cx.ntb, cx.N, cx.mode, cx.nseq, cx.L = 128, 4, TN, "prompt", 1, TN
        N = TN
        s_, b_, slot = j % 4, j // 4, j % 2
        def load_x(jj, tb):
            P.dma("sp", xres[0][:, tb, :], x_d[jj * TN + tb * 128:jj * TN + (tb + 1) * 128, :], writes=xk(tb))
        if j == 0 or not PRELOAD[0]:
            for tb in range(4):
                load_x(j, tb)
        if j == 0:
            cache_shift_copies()
        CK(f"t{j}_load")
        norm_to_hT(cx, R_G1)
        CK(f"t{j}_norm1")
        ffn(cx, w1gu_d, w1d_d)
        CK(f"t{j}_ffn1")
        if j == 0:
            setup_bias()
        norm_to_hT(cx, R_GM)
        for _ in proj_fm(cx, YR0, 768, lambda c, b: act(actc(c)[:, 0:N], ps[b][:, 0:N], AF.Gelu_apprx_tanh, [("ps", b)], [("actT", c)])):
            pass
        s = wload([(win_d[:, V0c:V0c + 256], 0, 8, 256, 128)])
        for tbp in range(2):
            b = genbank()
            for t2 in range(2):
                tb = 2 * tbp + t2
                for kc in range(8):
                    mm(ps[b][:, t2 * 256:(t2 + 1) * 256], hT[:, kc, tb * 128:(tb + 1) * 128], wsl[s][:, kc, 0:256], kc == 0, kc == 7, wk(s) + [("hT", kc)], [("ps", b)])
            evac(V0[:, slot, 2 * tbp:2 * tbp + 2, :], ps[b][:, :].rearrange("p (t c) -> p t c", t=2), [("ps", b)], [("V0", slot)])
            if j == NT - 1 and tbp == 1:
                si = nscr()
                dve_copy(scr[si][:, 0:256], ps[b][:, 256:512], [("ps", b)], [("scr", si)])
                P.dma("sp", pv_d[0][:, :], scr[si][:, 0:256], reads=[("scr", si)])
        s = wload([(win_d[:, V0c + 256:V0c + 768], 0, 8, 512, 128)])
        for r in range(4):
            b = genbank()
            for kc in range(8):
                lt = hT[:, kc, :].rearrange("p (i r) -> p r i", r=4)[:, r, :]
                mm(ps[b][:, :], lt, wsl[s][:, kc, 0:512], kc == 0, kc == 7, wk(s) + [("hT", kc)], [("ps", b)])
            evac(V1[:, slot, r, :], ps[b][:, 0:256], [("ps", b)], [("V1", slot)])
            evac(V2s[:, r, :], ps[b][:, 256:512], [("ps", b)], [("V2s", r)])
            if j >= 4:
                si = nscr()
                dve_copy(scr[si][:, :], ps[b][:, :], [("ps", b)], [("scr", si)])
                if j == NT - 1:
                    P.dma("sp", pv_d[1][r::4, :], scr[si][:, 0:256], reads=[("scr", si)])
                P.dma("sp", pv_d[2][(j - 4) * TN + r:(j - 3) * TN:4, :], scr[si][:, 256:512], reads=[("scr", si)])
        for q4 in range(4):
            P.dma("sp", V2[32 * s_:32 * s_ + 32, b_, 4 * q4:4 * q4 + 4, :], V2s[q4::4, :, :], reads=[("V2s", r) for r in range(4)], writes=[("V2", b_)])
        def k_consume(c, b):
            if c < 4:
                evac(k01[:, c, slot, :], ps[b][:, 0:N], [("ps", b)], [("k01", c, slot)])
            else:
                evac(k2[:, c - 4, j * TN:(j + 1) * TN], ps[b][:, 0:N], [("ps", b)], [("k2", c - 4, j)])
        for (s, done, n) in proj_fm(cx, K0, 768, k_consume):
            if j >= 4:
                ch = done // 3
                for tb in range(4):
                    b = genbank()
                    for kc in range(8):
                        mm(ps[b][:, 0:384], hT[:, kc, tb * 128:(tb + 1) * 128], wsl[s][:, kc, 0:384], kc == 0, kc == 7, wk(s) + [("hT", kc)], [("ps", b)])
                    si = nscr()
                    dve_copy(scr[si][:, 0:384], ps[b][:, 0:384], [("ps", b)], [("scr", si)])
                    if ch == 0:
                        if j == NT - 1:
                            if tb == 3:
                                P.dma("sp", pk_d[0][:, :], scr[si][:, 0:256], reads=[("scr", si)])
                            P.dma("sp", pk_d[1][tb * 128:(tb + 1) * 128, 0:128], scr[si][:, 256:384], reads=[("scr", si)])
                    else:
                        if j == NT - 1:
                            P.dma("sp", pk_d[1][tb * 128:(tb + 1) * 128, 128:256], scr[si][:, 0:128], reads=[("scr", si)])
                        P.dma("sp", pk_d[2][(j - 4) * TN + tb * 128:(j - 4) * TN + (tb + 1) * 128, :], scr[si][:, 128:384], reads=[("scr", si)])
        for _ in proj_fm(cx, Q0, 768, lambda c, b: act(QM[:, c, 0:N], ps[b][:, 0:N], AF.Copy, [("ps", b)], [("QM", c)], scale=0.125)):
            pass
        for (s, done, n) in proj_fm(cx, XR0, 768, lambda c, b: evac(XR[:, c, 3:3 + N], ps[b][:, 0:N], [("ps", b)], [("XR", c)])):
            if j == NT - 1:
                b = genbank()
                for kc in range(8):
                    mm(ps[b][0:3, 0:384], hT[:, kc, N - 3:N], wsl[s][:, kc, 0:384], kc == 0, kc == 7, wk(s) + [("hT", kc)], [("ps", b)])
                si = nscr()
                dve_copy(scr[si][0:3, 0:384], ps[b][0:3, 0:384], [("ps", b)], [("scr", si)])
                P.dma("sp", pconv_d[:, 128 * done:128 * done + 384], scr[si][0:3, 0:384], reads=[("scr", si)])
        for _ in proj_fm(cx, QM0, 512, lambda c, b: evac(qmT[:, c, 0:N], ps[b][:, 0:N], [("ps", b)], [("qmT", c)])):
            pass
        CK(f"t{j}_win")
        attention_prompt(j, hooks=lru_p1_slots(cx))
        CK(f"t{j}_attn")
        mem_attn(cx, lambda bi: mkT, lambda bi: mvb, [(0, N, 0)])
        CK(f"t{j}_memattn")
        lru_p2a(cx)
        CK(f"t{j}_lru")
        merge_and_out(cx, hooks=[(lambda c=c: lru_p2b(cx, c)) for c in range(6)])
        if j == NT - 1:
            b = genbank()
            mm(ps[b][0:6, 0:128], hst[:, :, 0], identf[:, :], True, True, [("hst", c) for c in range(6)] + ["identf"], [("ps", b)])
            si = nscr()
            dve_copy(scr[si][0:6, 0:128], ps[b][0:6, 0:128], [("ps", b)], [("scr", si)])
            P.dma("sp", plru_d[:, :], scr[si][0:6, 0:128], reads=[("scr", si)])
        CK(f"t{j}_merge")
        norm_to_hT(cx, R_G2)
        ffn(cx, w2gu_d, w2d_d)
        nxt = (lambda tb: load_x(j + 1, tb)) if (j + 1 < ntiles) else None
        PRELOAD[0] = nxt is not None
        final_norm_store(cx, y_d[j * TN:(j + 1) * TN, :], after_tb=nxt)

    try:
        setup()
        CK("setup")
        if ntiles == 0:
            cache_shift_copies()
        prompt_memory()
        CK("mem")
        if ntiles == 0:
            setup_bias()
        for j in range(ntiles):
            prompt_tile(j)
            CK(f"t{j}")
        CK("prompt")
        sample_phase()
    except _Stop:
        pass
    if upto is not None:
        dbg_x = dout("dbg_x", [512, D]); dbg_at = dout("dbg_at", [128, 22 * 512]); dbg_qm = dout("dbg_qm", [128, 4096]); dbg_ht = dout("dbg_ht", [128, 4096])
        P.dma("sp", dbg_x.rearrange("(tb p) d -> p tb d", p=128), xres[0][:, :, :], reads=[k for t in range(4) for k in xk(t)])
        dbg_bias = dout("dbg_bias", [128, 3072])
        P.dma("sp", dbg_bias, biasT[:, :, :, :, :].rearrange("p g t h q -> p (g t h q)"), reads=[("biasT", g, t) for g in range(3) for t in range(2)])
        P.dma("pool", dbg_at, AT[:, :], reads=[("actT", c) for c in range(22)], lanes=["dbg0"])
        P.dma("pool", dbg_qm, QM[:, :, :].rearrange("p c n -> p (c n)"), reads=[("QM", c) for c in range(8)], lanes=["dbg1"])
        P.dma("pool", dbg_ht, hT[:, :, :].rearrange("p c n -> p (c n)"), reads=[("hT", c) for c in range(8)], lanes=["dbg2"])

    P.emit(st)
    return nc, P, st


_CACHE = {}


def _get_program():
    if "nc" not in _CACHE:
        nc, P, st = build_program()
        _CACHE["nc"] = nc
        _CACHE["st"] = st
        _CACHE["P"] = P
    return _CACHE["nc"]


def make_in_maps(inp, cores=range(8)):
    f = lambda a: np.ascontiguousarray(np.asarray(a, dtype=np.float32))
    vec = lambda a: f(a).reshape(-1, 128)
    vecrows = np.concatenate([
        vec(inp["ffn1_norm"][0]), vec(inp["mix_norm"][0]), vec(inp["ffn2_norm"][0]), vec(inp["mem_norm"][0]),
        vec(inp["b_gate"][0]), vec(inp["conv_w"][0]), vec(inp["conv_b"][0]), vec(inp["lru_b_a"][0]),
        vec(inp["lru_b_i"][0]), vec(inp["lru_lambda"][0])], axis=0)
    assert vecrows.shape == (NROWS, 128)
    shared = {
        "rel_bias": f(inp["rel_bias"]), "vecrows": f(vecrows), "final_norm": f(inp["final_norm"]).reshape(1, D),
        "ffn1_w_gu": f(inp["ffn1_w_gu"][0]), "ffn1_w_down": f(inp["ffn1_w_down"][0]), "w_in": f(inp["w_in"][0]),
        "lru_w_a": f(inp["lru_w_a"][0]), "lru_w_i": f(inp["lru_w_i"][0]), "w_mem_kv": f(inp["w_mem_kv"][0]),
        "w_att_o": f(inp["w_att_o"][0]), "w_rec_o": f(inp["w_rec_o"][0]), "w_mem_o": f(inp["w_mem_o"][0]),
        "w_out": f(inp["w_out"][0]), "ffn2_w_gu": f(inp["ffn2_w_gu"][0]), "ffn2_w_down": f(inp["ffn2_w_down"][0]),
        "identf": np.eye(128, dtype=np.float32), "jflip": np.ascontiguousarray(np.eye(128, dtype=np.float32)[::-1]),
        "oh": _onehot_const(),
    }
    in_maps = []
    for c in cores:
        bs = slice(NSB * c, NSB * (c + 1))
        m = dict(shared)
        m["x"] = f(inp["x_prompt"][c]); m["xs"] = f(inp["x_sample"][bs]).reshape(NS, D); m["mem"] = f(inp["mem_prompt"][c])
        for g in range(3):
            m[f"ck{g}"] = f(inp[f"cache_win_k{g}"][0, bs]).reshape(NSB, -1, 256)
            m[f"cv{g}"] = f(inp[f"cache_win_v{g}"][0, bs]).reshape(NSB, -1, 256)
        m["sconv"] = f(inp["state_conv"][0, bs]).reshape(NSB * 3 * 6, 128)
        m["slru"] = f(inp["state_lru"][0, bs]).reshape(NSB * 6, 128)
        m["cmk"] = f(inp["cache_mem_k"][0, bs]).reshape(NSB, 256, 512)
        m["cmv"] = f(inp["cache_mem_v"][0, bs]).reshape(NSB, 256, 512)
        in_maps.append(m)
    return in_maps


def kernel(**inp):
    nc = _get_program()
    n = 8
    in_maps = make_in_maps(inp)
    res = run_bass_kernel_spmd(nc, in_maps, core_ids=list(range(n)))
    R = res.results
    cat = lambda k: np.stack([np.asarray(R[c][k], dtype=np.float32) for c in range(n)], axis=0)
    y_prompt = cat("y")
    y_sample = cat("ys").reshape(32, 8, D)
    outs = [y_prompt, y_sample]
    for g in range(3):
        L = min(GROUPS[g][0], SEQ)
        outs.append(cat(f"pk{g}").reshape(1, 8, L, 4, 64))
        outs.append(cat(f"pv{g}").reshape(1, 8, L, 4, 64))
    outs.append(cat("pconv").reshape(1, 8, 3, 768))
    outs.append(cat("plru").reshape(1, 8, 768))
    outs.append(cat("pmk").reshape(1, 8, 256, 4, 128))
    outs.append(cat("pmv").reshape(1, 8, 256, 4, 128))
    for g in range(3):
        L = GROUPS[g][0]
        outs.append(cat(f"sk{g}").reshape(1, 32, L, 4, 64))
        outs.append(cat(f"sv{g}").reshape(1, 32, L, 4, 64))
    outs.append(cat("sconvo").reshape(1, 32, 3, 768))
    outs.append(cat("slruo").reshape(1, 32, 768))
    return tuple(outs)
```
